# Optimizing a Trainium2 kernel written in Bass

```python
import jax, jax.numpy as jnp
from jax import lax
import numpy as np

D_MODEL = 4096
BATCH = 1
SEQ = 8192
DEPTH = 1

HEAD_DIM = 128
N_Q_HEADS = D_MODEL // HEAD_DIM
N_KV_HEADS = max(N_Q_HEADS // 4, 1)
GROUP = N_Q_HEADS // N_KV_HEADS
ATTN_WIDTH = N_Q_HEADS * HEAD_DIM
KV_WIDTH = N_KV_HEADS * HEAD_DIM
WINDOW = 128
BLOCK = 128
ROPE_THETA = 500000.0
ROT_DIM = HEAD_DIM // 4
CONV_WIDTH = D_MODEL
CONV_K = 3
RMS_EPS = 1e-6

Q_END = ATTN_WIDTH
K_END = Q_END + KV_WIDTH
V_END = K_END + KV_WIDTH
AG_END = V_END + ATTN_WIDTH
CB_END = AG_END + CONV_WIDTH
CC_END = CB_END + CONV_WIDTH
CX_END = CC_END + CONV_WIDTH
CG_END = CX_END + CONV_WIDTH
MA_END = CG_END + D_MODEL
MB_END = MA_END + D_MODEL
IN_COLS = MB_END

kernel_name = "hybrid_gated_swa_shortconv_encoder"


def rms_norm(x, gain):
    x32 = x.astype(jnp.float32)
    y = x32 * lax.rsqrt(jnp.mean(x32 * x32, axis=-1, keepdims=True) + RMS_EPS)
    return (y * gain.astype(jnp.float32)).astype(x.dtype)


def partial_rotary(t, cos, sin):
    half = ROT_DIM // 2
    t32 = t[..., :ROT_DIM].astype(jnp.float32)
    t1, t2 = t32[..., :half], t32[..., half:]
    c, s = cos[None, :, None, :], sin[None, :, None, :]
    rot = jnp.concatenate([t1 * c - t2 * s, t2 * c + t1 * s], axis=-1).astype(t.dtype)
    return jnp.concatenate([rot, t[..., ROT_DIM:]], axis=-1)


def windowed_gqa_sink(q, k, v, sink):
    b, s = q.shape[0], q.shape[1]
    nb = s // BLOCK
    qb = q.reshape(b, nb, BLOCK, N_KV_HEADS, GROUP, HEAD_DIM)

    def band(t):
        tp = jnp.pad(t, ((0, 0), (BLOCK, BLOCK), (0, 0), (0, 0)))
        tp = tp.reshape(b, nb + 2, BLOCK, N_KV_HEADS, HEAD_DIM)
        return jnp.concatenate([tp[:, :-2], tp[:, 1:-1], tp[:, 2:]], axis=2)

    kb, vb = band(k), band(v)
    scale = HEAD_DIM ** -0.5
    scores = jnp.einsum('bnqhgd,bnkhd->bnhgqk', qb, kb,
                        preferred_element_type=jnp.float32) * scale
    blk = jnp.arange(nb)[:, None, None]
    qpos = blk * BLOCK + jnp.arange(BLOCK)[None, :, None]
    kpos = (blk - 1) * BLOCK + jnp.arange(3 * BLOCK)[None, None, :]
    valid = (jnp.abs(kpos - qpos) <= WINDOW) & (kpos >= 0) & (kpos < s)
    scores = jnp.where(valid[None, :, None, None], scores, -jnp.inf)
    sink_b = sink.astype(jnp.float32).reshape(N_KV_HEADS, GROUP)[None, None, :, :, None, None]
    m = jnp.maximum(jnp.max(scores, axis=-1, keepdims=True), sink_b)
    p = jnp.exp(scores - m)
    denom = jnp.sum(p, axis=-1, keepdims=True) + jnp.exp(sink_b - m)
    probs = (p / denom).astype(v.dtype)
    out = jnp.einsum('bnhgqk,bnkhd->bnqhgd', probs, vb)
    return out.reshape(b, s, ATTN_WIDTH)


def centred_short_conv(u, w, bias):
    up = jnp.pad(u, ((0, 0), (1, 1), (0, 0)))
    return up[:, :-2] * w[0] + up[:, 1:-1] * w[1] + up[:, 2:] * w[2] + bias


def setup_inputs(seed: int = 0) -> dict:
    key = jax.random.key(seed)
    ks = jax.random.split(key, 12)
    f32 = jnp.float32
    x = jax.random.normal(ks[0], (BATCH, SEQ, D_MODEL), f32)
    norm_pre = 1.0 + 0.05 * jax.random.normal(ks[1], (DEPTH, D_MODEL), f32)
    w_in = jax.random.normal(ks[2], (DEPTH, D_MODEL, IN_COLS), f32) * D_MODEL ** -0.5
    b_merge = 0.1 * jax.random.normal(ks[3], (DEPTH, 2 * D_MODEL), f32)
    attn_sink = 0.5 * jax.random.normal(ks[4], (DEPTH, N_Q_HEADS), f32)
    conv_w = jax.random.normal(ks[5], (DEPTH, CONV_K, CONV_WIDTH), f32) * CONV_K ** -0.5
    conv_b = 0.05 * jax.random.normal(ks[6], (DEPTH, CONV_WIDTH), f32)
    w_attn_out = jax.random.normal(ks[7], (DEPTH, ATTN_WIDTH, D_MODEL), f32) * ATTN_WIDTH ** -0.5
    w_conv_out = jax.random.normal(ks[8], (DEPTH, CONV_WIDTH, D_MODEL), f32) * CONV_WIDTH ** -0.5
    w_out = jax.random.normal(ks[9], (DEPTH, D_MODEL, D_MODEL), f32) * D_MODEL ** -0.5
    norm_post = 1.0 + 0.05 * jax.random.normal(ks[10], (DEPTH, D_MODEL), f32)
    return {"x": x, "norm_pre": norm_pre, "w_in": w_in, "b_merge": b_merge,
            "attn_sink": attn_sink, "conv_w": conv_w, "conv_b": conv_b,
            "w_attn_out": w_attn_out, "w_conv_out": w_conv_out, "w_out": w_out,
            "norm_post": norm_post}


def reference(x, norm_pre, w_in, b_merge, attn_sink, conv_w, conv_b,
              w_attn_out, w_conv_out, w_out, norm_post):
    b, s, _ = x.shape
    pos = jnp.arange(s, dtype=jnp.float32)
    inv_freq = ROPE_THETA ** (-jnp.arange(0, ROT_DIM, 2, dtype=jnp.float32) / ROT_DIM)
    ang = pos[:, None] * inv_freq[None, :]
    cos, sin = jnp.cos(ang), jnp.sin(ang)

    for l in range(DEPTH):
        h = rms_norm(x, norm_pre[l])
        p = h @ w_in[l]
        q = p[..., :Q_END].reshape(b, s, N_Q_HEADS, HEAD_DIM)
        k = p[..., Q_END:K_END].reshape(b, s, N_KV_HEADS, HEAD_DIM)
        v = p[..., K_END:V_END].reshape(b, s, N_KV_HEADS, HEAD_DIM)
        attn_gate = p[..., V_END:AG_END]
        conv_bg = p[..., AG_END:CB_END]
        conv_cg = p[..., CB_END:CC_END]
        conv_x = p[..., CC_END:CX_END]
        conv_gate = p[..., CX_END:CG_END]
        merge_logits = p[..., CG_END:MB_END] + b_merge[l]

        q = partial_rotary(q, cos.astype(jnp.float32), sin.astype(jnp.float32))
        k = partial_rotary(k, cos.astype(jnp.float32), sin.astype(jnp.float32))
        a = windowed_gqa_sink(q, k, v, attn_sink[l])
        y_a = (a * jax.nn.silu(attn_gate)) @ w_attn_out[l]

        c = centred_short_conv(conv_cg * conv_x, conv_w[l], conv_b[l])
        y_b = (conv_bg * c * jax.nn.silu(conv_gate)) @ w_conv_out[l]

        g = jax.nn.sigmoid(merge_logits)
        m = g[..., :D_MODEL] * y_a + g[..., D_MODEL:] * y_b
        o = m @ w_out[l]
        x = x + rms_norm(o, norm_post[l])
    return x
```

```python
import numpy as np
import ml_dtypes
from contextlib import ExitStack

import concourse.bass as bass
import concourse.mybir as mybir
from concourse.bass_utils import run_bass_kernel_spmd

F32 = mybir.dt.float32
BF16 = mybir.dt.bfloat16
AF = mybir.ActivationFunctionType
ALU = mybir.AluOpType

D = 4096
S = 8192
NCORES = 8
TOK = S // NCORES
EXT = TOK + 256
KC = D // 128
HD = 128
NQH = 32
NKVH = 8
Q_END = 4096
K_END = Q_END + 1024
V_END = K_END + 1024
AG_END = V_END + 4096
CB_END = AG_END + 4096
CC_END = CB_END + 4096
CX_END = CC_END + 4096
CG_END = CX_END + 4096
MA_END = CG_END + 4096
IN_COLS = MA_END + 4096
EPS = 1e-6
SCALE = HD ** -0.5
NSLOT = 4

ENGS = ("pe", "act", "dve", "pool", "sp")


class Sem:
    def __init__(self, h):
        self.h = h
        self.count = 0


class Ctx:
    def __init__(self, nc, es):
        self.nc = nc
        self.es = es
        self.esem = {e: Sem(es.enter_context(nc.semaphore("sq_" + e))) for e in ("pe", "act", "dve", "pool")}
        self.waited = {e: {} for e in ENGS}
        self.ops = None
        self.nsem = 4

    def new_sem(self, name):
        self.nsem += 1
        return Sem(self.es.enter_context(self.nc.semaphore(name)))

    def begin(self):
        self.ops = {e: [] for e in ENGS}

    def _waits(self, eng, waits):
        w = []
        best = {}
        for t in waits:
            if t is None:
                continue
            s, v = t
            if id(s) not in best or best[id(s)][1] < v:
                best[id(s)] = (s, v)
        for (s, v) in best.values():
            if self.waited[eng].get(id(s), 0) >= v:
                continue
            self.waited[eng][id(s)] = v
            w.append((s.h, v))
        return w

    def op(self, eng, fn, waits=(), signal=True):
        w = self._waits(eng, waits)
        inc = None
        ticket = None
        if signal:
            s = self.esem[eng]
            s.count += 1
            inc = (s.h, 1)
            ticket = (s, s.count)
        self.ops[eng].append((fn, w, inc))
        return ticket

    def dma(self, eng, fn, dsem, waits=()):
        w = self._waits(eng, waits)
        dsem.count += 16
        self.ops[eng].append((fn, w, (dsem.h, 16)))
        return (dsem, dsem.count)

    def wait_only(self, eng, waits):
        w = self._waits(eng, waits)
        if w:
            self.ops[eng].append((None, w, None))

    def run(self):
        nc = self.nc
        ops = self.ops

        def replay(e, lst):
            for fn, w, inc in lst:
                for (h, v) in w:
                    e.wait_ge(h, v)
                if fn is None:
                    continue
                ins = fn(e)
                if inc is not None:
                    ins.then_inc(inc[0], inc[1])

        with nc.Block() as block:
            @block.tensor
            def _(e):
                replay(e, ops["pe"])

            @block.scalar
            def _(e):
                replay(e, ops["act"])

            @block.vector
            def _(e):
                replay(e, ops["dve"])

            @block.gpsimd
            def _(e):
                replay(e, ops["pool"])

            @block.sync
            def _(e):
                replay(e, ops["sp"])
        self.ops = None


class WRing:
    def __init__(self, cx, slots):
        self.cx = cx
        self.slots = slots
        self.lds = [cx.new_sem("wld%d" % i) for i in range(len(slots))]
        self.free = [None] * len(slots)
        self.n = 0
        self.pref = []

    def prefetch(self, wview, c0):
        i, t = self._load(wview, c0)
        self.pref.append(((id(wview), c0), i, t))

    def load(self, wview, c0):
        if self.pref:
            key, i, t = self.pref.pop(0)
            assert key == (id(wview), c0), "ring prefetch order mismatch"
            return i, t
        return self._load(wview, c0)

    def _load(self, wview, c0):
        i = self.n % len(self.slots)
        self.n += 1
        slot = self.slots[i]
        c0 = c0 % wview.shape[2]
        src = wview[:, :, c0:c0 + 128]
        t = self.cx.dma("pool", lambda e, slot=slot, src=src: e.dma_start(out=slot[:], in_=src),
                        self.lds[i], waits=[self.free[i]])
        return i, t

    def release(self, i, ticket):
        self.free[i] = ticket


def proj_task(cx, ring, wview, c0, rhs_fn, chunks, banks, bank_free, inter=None, every=4, kc_waits=None):
    si, lt = ring.load(wview, c0)
    slot = ring.slots[si]
    first = True
    ticket = None
    nmm = KC * len(chunks)
    k = 0
    for kc in range(KC):
        for ci, (tok0, n, bi, col0) in enumerate(chunks):
            k += 1
            last = (k == nmm)
            out = banks[bi][:, col0:col0 + n]
            lhsT = slot[:, kc, :]
            rhs = rhs_fn(kc, tok0, n)
            waits = ([lt] + list(bank_free)) if first else []
            if kc_waits is not None and ci == 0 and kc in kc_waits:
                waits = list(waits) + list(kc_waits[kc])
            first = False
            fn = (lambda e, out=out, lhsT=lhsT, rhs=rhs, st=(kc == 0), sp=(kc == KC - 1):
                  e.matmul(out, lhsT, rhs, start=st, stop=sp))
            t = cx.op("pe", fn, waits=waits, signal=last)
            if last:
                ticket = t
        if inter is not None and kc % every == every - 1 and kc != KC - 1:
            next(inter, None)
    ring.release(si, ticket)
    return ticket


def build_nc(debug=False, stop_after=None, small_w=False, bis=99):
    nc = bass.Bass("TRN2", target_bir_lowering=False)
    dk = "ExternalOutput" if debug else "Internal"
    x_ext = nc.dram_tensor("x_ext", [EXT, D], F32, kind="ExternalInput").ap()
    w_in = nc.dram_tensor("w_in", [D, IN_COLS] if not small_w else [D, 128], F32, kind="ExternalInput").ap()
    w_a = nc.dram_tensor("w_a", [D, D] if not small_w else [D, 128], F32, kind="ExternalInput").ap()
    w_b = nc.dram_tensor("w_b", [D, D] if not small_w else [D, 128], F32, kind="ExternalInput").ap()
    w_o = nc.dram_tensor("w_o", [D, D] if not small_w else [D, 128], F32, kind="ExternalInput").ap()
    gpre_d = nc.dram_tensor("gpre", [128, KC], F32, kind="ExternalInput").ap()
    bm_d = nc.dram_tensor("bm", [128, 64], F32, kind="ExternalInput").ap()
    sink_d = nc.dram_tensor("sinkb", [128, NQH], F32, kind="ExternalInput").ap()
    convw_d = nc.dram_tensor("convw", [128, 3 * KC], F32, kind="ExternalInput").ap()
    convb_d = nc.dram_tensor("convb", [128, KC], F32, kind="ExternalInput").ap()
    gpost_d = nc.dram_tensor("gpost", [128, D], F32, kind="ExternalInput").ap()
    gprebc_d = nc.dram_tensor("gprebc", [128, D], F32, kind="ExternalInput").ap()
    identf_d = nc.dram_tensor("identf", [128, 128], F32, kind="ExternalInput").ap()
    identb_d = nc.dram_tensor("identb", [128, 128], BF16, kind="ExternalInput").ap()
    onesb_d = nc.dram_tensor("onesb", [128, 128], BF16, kind="ExternalInput").ap()
    masks_d = nc.dram_tensor("masks", [128, 4 * 128], BF16, kind="ExternalInput").ap()
    rm_d = nc.dram_tensor("rmat", [128, 128], F32, kind="ExternalInput").ap()
    cos_d = nc.dram_tensor("cosT", [32, EXT], F32, kind="ExternalInput").ap()
    sin_d = nc.dram_tensor("sinT", [32, EXT], F32, kind="ExternalInput").ap()
    out_d = nc.dram_tensor("out", [TOK, D], F32, kind="ExternalOutput").ap()
    ag_sp = nc.dram_tensor("ag_sp", [KC, 128, TOK], BF16, kind=dk).ap()
    cb_sp = nc.dram_tensor("cb_sp", [KC, 128, TOK], BF16, kind=dk).ap()
    ma_sp = nc.dram_tensor("ma_sp", [KC, 128, TOK], F32, kind=dk).ap()
    m_sp = nc.dram_tensor("m_sp", [KC, 128, TOK], BF16, kind=dk).ap()
    dbg = {}
    if debug:
        dbg["hT"] = nc.dram_tensor("dbg_hT", [128, KC * EXT], BF16, kind="ExternalOutput").ap()
        dbg["kT"] = nc.dram_tensor("dbg_kT", [128, NKVH * EXT], BF16, kind="ExternalOutput").ap()
        dbg["v"] = nc.dram_tensor("dbg_v", [128, 10 * 1024], BF16, kind="ExternalOutput").ap()

    wv_in = w_in.rearrange("(kc p) c -> p kc c", p=128)
    wv_a = w_a.rearrange("(kc p) c -> p kc c", p=128)
    wv_b = w_b.rearrange("(kc p) c -> p kc c", p=128)
    wv_o = w_o.rearrange("(kc p) c -> p kc c", p=128)

    with ExitStack() as es:
        cx = Ctx(nc, es)
        sb = lambda name, shape, dt: es.enter_context(nc.sbuf_tensor(name, shape, dt))
        wslots = [sb("wr%d" % i, [128, KC, 128], BF16) for i in range(NSLOT)]
        gpre = sb("gpre_s", [128, KC], F32)
        bm = sb("bm_s", [128, 64], F32)
        sinkb = sb("sink_s", [128, NQH], F32)
        esink = sb("esink_s", [128, NQH], F32)
        convw = sb("convw_s", [128, 3 * KC], F32)
        convb = sb("convb_s", [128, KC], F32)
        identf = sb("identf_s", [128, 128], F32)
        identb = sb("identb_s", [128, 128], BF16)
        onesb = sb("onesb_s", [128, 128], BF16)
        masks = sb("masks_s", [128, 512], BF16)
        rmat = sb("rmat_s", [128, 128], F32)
        ssq = sb("ssq_s", [128, 8 * KC], F32)
        rstd_o = sb("rstd_o", [128, 8], F32)
        banks = [es.enter_context(nc.psum_tensor("bank%d" % i, [128, 512], F32)) for i in range(8)]
        ring = WRing(cx, wslots)
        csem = cx.new_sem("const")
        dA = cx.new_sem("dA")
        dB = cx.new_sem("dB")
        dC = cx.new_sem("dC")
        dD = cx.new_sem("dD")
        ldq = [cx.new_sem("ldq%d" % i) for i in range(8)]
        spl = [cx.new_sem("spl%d" % i) for i in range(2)]
        hs = es.enter_context(ExitStack())
        hT = hs.enter_context(nc.sbuf_tensor("hT", [128, KC, EXT], BF16))

        def fl(*ts):
            o = []
            for t in ts:
                if t is None:
                    continue
                if isinstance(t, list):
                    o.extend(fl(*t))
                else:
                    o.append(t)
            return o

        OWN2 = [(128, 512), (640, 512)]
        rhs_h = lambda kc, tok0, n: hT[:, kc, tok0:tok0 + n]
        nphase = stop_after if stop_after is not None else 99

        with ExitStack() as ps:
            psb = lambda name, shape, dt: ps.enter_context(nc.sbuf_tensor(name, shape, dt))
            xt = [psb("xt%d" % i, [128, D], F32) for i in range(2)]
            hnb = [psb("hnb%d" % i, [128, D], BF16) for i in range(2)]
            junk = psb("junk0", [128, D], BF16)
            gbc = psb("gbc", [128, D], F32)
            st = psb("st0", [128, 40], F32)
            cx.begin()
            consts = [(gpre, gpre_d), (bm, bm_d), (sinkb, sink_d), (convw, convw_d), (convb, convb_d),
                      (identf, identf_d), (identb, identb_d), (onesb, onesb_d), (masks, masks_d), (rmat, rm_d)]
            ct = None
            for (s_t, d_t) in consts:
                ct = cx.dma("sp", lambda e, s_t=s_t, d_t=d_t: e.dma_start(out=s_t[:], in_=d_t), csem)
            tgb = cx.dma("sp", lambda e: e.dma_start(out=gbc[:], in_=gprebc_d), dC)
            xsem = [dA, dB]
            xt_free = [None, None]
            hn_free = [None, None]
            bank_free = [None] * 8
            bbf = [banks[i][:].bitcast(BF16) for i in range(8)]
            junk_free = None
            for t in range(10):
                b = t % 2
                lt = cx.dma("sp", lambda e, b=b, t=t: e.dma_start(out=xt[b][:], in_=x_ext[t * 128:(t + 1) * 128, :]),
                            xsem[b], waits=fl(xt_free[b]))
                t_sq = cx.op("act", lambda e, b=b, t=t: e.activation(junk[:], xt[b][:], AF.Square,
                                                                      accum_out=st[:, t:t + 1]), waits=fl(lt, junk_free))
                junk_free = t_sq
                t1 = cx.op("dve", lambda e, t=t: e.tensor_scalar(st[:, 10 + t:11 + t], st[:, t:t + 1], 1.0 / D, EPS,
                                                                  ALU.mult, ALU.add), waits=[t_sq])
                t2 = cx.op("act", lambda e, t=t: e.activation(st[:, 20 + t:21 + t], st[:, 10 + t:11 + t], AF.Sqrt),
                           waits=[t1])
                t3 = cx.op("dve", lambda e, t=t: e.reciprocal(st[:, 30 + t:31 + t], st[:, 20 + t:21 + t]), waits=[t2])
                t4 = cx.op("dve", lambda e, b=b, t=t: e.scalar_tensor_tensor(
                    hnb[b][:], xt[b][:], st[:, 30 + t:31 + t], gbc[:], ALU.mult, ALU.mult),
                    waits=fl(t3, lt, tgb, hn_free[b]))
                xt_free[b] = [t4, t_sq]
                tp = None
                for g in range(4):
                    bi = b * 4 + g
                    for i in range(8):
                        kc = g * 8 + i
                        w = fl(t4, ct, bank_free[bi]) if i == 0 else ()
                        tp = cx.op("pe", lambda e, bi=bi, i=i, b=b, kc=kc: e.transpose(
                            bbf[bi][:, i * 128:(i + 1) * 128], hnb[b][:, kc * 128:(kc + 1) * 128], identb[:]),
                            waits=w, signal=(i == 7))
                    src = bbf[bi][:, :].rearrange("p (k d) -> p k d", d=128)
                    dst = hT[:, g * 8:(g + 1) * 8, t * 128:(t + 1) * 128]
                    if g % 2 == 0:
                        ev = cx.op("act", lambda e, src=src, dst=dst: e.activation(dst, src, AF.Copy), waits=[tp])
                    else:
                        ev = cx.op("dve", lambda e, src=src, dst=dst: e.tensor_copy(dst, src), waits=[tp])
                    bank_free[bi] = [ev]
                hn_free[b] = tp
            cx.wait_only("sp", fl(xt_free, ct))
            cx.wait_only("pe", fl(bank_free))
            if debug:
                dt_ = cx.dma("sp", lambda e: e.dma_start(out=dbg["hT"], in_=hT[:].rearrange("p k t -> p (k t)")),
                             dC, waits=fl(bank_free))
                cx.wait_only("sp", [dt_])
            for g_ in range(3):
                ring.prefetch(wv_in, Q_END + g_ * 128)
            cx.run()
        if nphase <= 0:
            return nc

        with ExitStack() as att:
            asb = lambda name, shape, dt: att.enter_context(nc.sbuf_tensor(name, shape, dt))
            kT = asb("kT", [128, NKVH, EXT], BF16)
            vtok = asb("vtok", [128, 10, 1024], BF16)
            cosT = asb("cosT_s", [32, EXT], F32)
            sinS = asb("sinS_s", [32, EXT], F32)
            with ExitStack() as ps:
                psb = lambda name, shape, dt: ps.enter_context(nc.sbuf_tensor(name, shape, dt))
                k32 = psb("k32", [32, EXT], F32)
                ksw = psb("ksw", [32, EXT], F32)
                kt1 = psb("kt1", [32, EXT], F32)
                kt2 = psb("kt2", [32, EXT], F32)
                vT_sb = [psb("vTsb%d" % i, [128, EXT], BF16) for i in range(2)]
                cx.begin()
                tc1 = cx.dma("sp", lambda e: e.dma_start(out=cosT[:], in_=cos_d), csem)
                tc2 = cx.dma("sp", lambda e: e.dma_start(out=sinS[:], in_=sin_d), csem)
                sets = [(0, 1, 2), (3, 4, 5)]
                set_free = [None, None]
                CH3 = [(0, 512), (512, 512), (1024, 256)]
                pending = None
                k_post = None
                tp_free = None
                vsb_free = [None, None]
                b6 = banks[6][:].bitcast(BF16)
                b7 = banks[7][:].bitcast(BF16)
                last = []
                for ti in range(16):
                    isk = ti < 8
                    g = ti % 8
                    bs = sets[ti % 2]
                    c0 = (Q_END if isk else K_END) + g * 128
                    chunks = [(tok0, n, bs[i], 0) for i, (tok0, n) in enumerate(CH3)]
                    tk = proj_task(cx, ring, wv_in, c0, rhs_h, chunks, banks, fl(set_free[ti % 2]))
                    if pending is not None:
                        pending()
                        pending = None
                    if isk:
                        ea = ed = None
                        for i, (tok0, n) in enumerate(CH3):
                            ea = cx.op("act", lambda e, g=g, i=i, tok0=tok0, n=n, bs=bs: e.activation(
                                kT[:, g, tok0:tok0 + n], banks[bs[i]][:, 0:n], AF.Copy), waits=[tk])
                            if bis >= 11:
                                ed = cx.op("act", lambda e, i=i, tok0=tok0, n=n, bs=bs: e.activation(
                                    k32[0:32, tok0:tok0 + n], banks[bs[i]][0:32, 0:n], AF.Copy),
                                    waits=fl(tk, k_post))
                        set_free[ti % 2] = [ea, ed]
                        if bis < 12 or bis == 15:
                            last = [ea, ed]
                            continue
                        d1 = cx.dma("sp", lambda e: e.dma_start(out=ksw[0:16, :], in_=k32[16:32, :]), dC,
                                    waits=fl(ed, k_post))
                        d2 = cx.dma("sp", lambda e: e.dma_start(out=ksw[16:32, :], in_=k32[0:16, :]), dC,
                                    waits=fl(ed, k_post))
                        o1 = cx.op("dve", lambda e: e.tensor_tensor(kt1[:], k32[:], cosT[:], ALU.mult),
                                   waits=fl(ed, tc1, tc2, k_post))
                        o2 = cx.op("dve", lambda e: e.tensor_tensor(kt2[:], ksw[:], sinS[:], ALU.mult),
                                   waits=fl(d1, d2, tc2, k_post))
                        o3 = cx.op("dve", lambda e, g=g: e.tensor_tensor(kT[0:32, g, :], kt1[:], kt2[:], ALU.add),
                                   waits=[o1, o2, ea])
                        k_post = o3
                        last = [o3, ea]
                    else:
                        vb = ti % 2
                        ea = None
                        for i, (tok0, n) in enumerate(CH3):
                            ea = cx.op("act", lambda e, vb=vb, i=i, tok0=tok0, n=n, bs=bs: e.activation(
                                vT_sb[vb][:, tok0:tok0 + n], banks[bs[i]][:, 0:n], AF.Copy),
                                waits=fl(tk, vsb_free[vb]))
                        set_free[ti % 2] = [ea]
                        if bis < 13:
                            last = [ea]
                            continue

                        def post(g=g, vb=vb, ea=ea):
                            nonlocal tp_free, last
                            tpt = None
                            for t in range(10):
                                dst = (b6 if t < 5 else b7)[:, (t % 5) * 128:(t % 5 + 1) * 128]
                                tpt = cx.op("pe", lambda e, dst=dst, t=t, vb=vb: e.transpose(
                                    dst, vT_sb[vb][:, t * 128:(t + 1) * 128], identb[:]),
                                    waits=fl(ea, tp_free) if t == 0 else (), signal=(t == 9))
                            vsb_free[vb] = tpt
                            if bis < 14:
                                tp_free = tpt
                                last = [tpt]
                                return
                            c1 = cx.op("dve", lambda e, g=g: e.tensor_copy(
                                vtok[:, 0:5, g * 128:(g + 1) * 128],
                                b6[:, 0:640].rearrange("p (t d) -> p t d", d=128)), waits=[tpt])
                            c2 = cx.op("dve", lambda e, g=g: e.tensor_copy(
                                vtok[:, 5:10, g * 128:(g + 1) * 128],
                                b7[:, 0:640].rearrange("p (t d) -> p t d", d=128)), waits=[tpt])
                            tp_free = c2
                            last = [c2]
                        pending = post
                if pending is not None:
                    pending()
                fin = fl(last, set_free)
                if debug:
                    da = cx.dma("sp", lambda e: e.dma_start(out=dbg["kT"], in_=kT[:].rearrange("p k t -> p (k t)")),
                                dC, waits=fin)
                    db = cx.dma("sp", lambda e: e.dma_start(out=dbg["v"], in_=vtok[:].rearrange("p k t -> p (k t)")),
                                dD, waits=fin)
                    cx.wait_only("sp", [da, db])
                ring.prefetch(wv_in, 0)
                ring.prefetch(wv_in, V_END)
                ring.prefetch(wv_in, 128)
                cx.run()
            if nphase <= 1:
                return nc

            with ExitStack() as ps:
                psb = lambda name, shape, dt: ps.enter_context(nc.sbuf_tensor(name, shape, dt))
                qT = [psb("qT%d" % i, [128, TOK], BF16) for i in range(2)]
                q32 = psb("q32", [128, TOK], F32)
                qsw = psb("qsw", [32, TOK], F32)
                sg = [psb("sg%d" % i, [128, TOK], F32) for i in range(2)]
                pT = [psb("pT%d" % i, [128, 384], BF16) for i in range(3)]
                sgt = psb("sgt", [128, TOK], F32)
                sgt_free = [None]
                deferred = []

                def flush_deferred():
                    while deferred:
                        deferred.pop(0)()

                def with_deferred(gen):
                    while True:
                        if deferred:
                            deferred.pop(0)()
                        if gen is not None:
                            try:
                                next(gen)
                            except StopIteration:
                                gen = None
                        yield
                rden = psb("rden", [128, 512], F32)
                agf = psb("agf", [128, 512], F32)
                agout = [psb("agout%d" % i, [128, TOK], BF16) for i in range(2)]
                cx.begin()
                t_es = cx.op("act", lambda e: e.activation(esink[:], sinkb[:], AF.Exp))
                A = (0, 1)
                B = (2, 3)
                A_free = None
                B_free = None
                q_ready = [None, None]
                q_free = [None, None]
                sg_ready = [None, None]
                sg_free = [None, None]
                rot_free = None
                s_free = [None, None]
                pT_free = [None, None, None]
                acc_free = None
                ag_free = [None, None]
                cnt = {"s": 0, "p": 0}
                spill_tix = []

                rden_free = None
                agf_free = None

                def attention(h):
                    nonlocal acc_free, rden_free, agf_free
                    g = h // 4
                    hb = h % 2
                    last_s = None
                    for hf in range(2):
                        tiles = list(range(4 * hf, 4 * hf + 6))
                        info = {}

                        def emit_S(t):
                            nonlocal last_s
                            n_lo = max(t - 2, 4 * hf)
                            n_hi = min(t, 4 * hf + 3)
                            N = (n_hi - n_lo + 1) * 128
                            si = cnt["s"] % 2
                            cnt["s"] += 1
                            pi = cnt["p"] % 3
                            cnt["p"] += 1
                            sb_ = banks[4 + si]
                            mlist = []
                            for n in range(n_lo, n_hi + 1):
                                mi = None
                                if n == t:
                                    mi = 0 if t == 0 else 1
                                elif n == t - 2:
                                    mi = 3 if t == 9 else 2
                                if mi is not None and bis != 23:
                                    mlist.append(((n - n_lo) * 128, mi))
                            ts = cx.op("pe", lambda e, sb_=sb_, t=t, n_lo=n_lo, N=N, nm=len(mlist): e.matmul(
                                sb_[:, 0:N], kT[:, g, t * 128:(t + 1) * 128], qT[hb][:, n_lo * 128:n_lo * 128 + N],
                                start=True, stop=True), waits=fl(q_ready[hb], s_free[si]), signal=(len(mlist) == 0))
                            for k_, (off, mi) in enumerate(mlist):
                                lastm = (k_ == len(mlist) - 1)
                                ts = cx.op("pe", lambda e, sb_=sb_, off=off, mi=mi, lastm=lastm: e.matmul(
                                    sb_[:, off:off + 128], identb[:], masks[:, mi * 128:(mi + 1) * 128],
                                    start=False, stop=True, skip_group_check=True), signal=lastm)
                            last_s = ts
                            te = cx.op("act", lambda e, sb_=sb_, pi=pi, N=N: e.activation(
                                pT[pi][:, 0:N], sb_[:, 0:N], AF.Exp, scale=SCALE), waits=fl(ts, pT_free[pi], t_es))
                            s_free[si] = te
                            tm = te
                            info[t] = (pi, n_lo, N, te, tm)

                        def emit_PV(t):
                            nonlocal acc_free
                            pi, n_lo, N, te, tm = info[t]
                            c0 = (n_lo - 4 * hf) * 128
                            first = (t == tiles[0])
                            lastt = (t == tiles[-1])
                            cx.op("pe", lambda e, t=t, pi=pi, c0=c0, N=N, first=first, lastt=lastt: e.matmul(
                                banks[6][:, c0:c0 + N], vtok[:, t, g * 128:(g + 1) * 128], pT[pi][:, 0:N],
                                start=first, stop=True, skip_group_check=True),
                                waits=fl(te, tm, acc_free if first else None), signal=False)
                            td = cx.op("pe", lambda e, pi=pi, c0=c0, N=N, first=first, lastt=lastt: e.matmul(
                                banks[7][:, c0:c0 + N], onesb[:], pT[pi][:, 0:N],
                                start=first, stop=True, skip_group_check=True), signal=True)
                            pT_free[pi] = td
                            return td

                        emit_S(tiles[0])
                        yield
                        td = None
                        for i, t in enumerate(tiles):
                            if i + 1 < len(tiles):
                                emit_S(tiles[i + 1])
                            td = emit_PV(t)
                            yield
                        flush_deferred()
                        e1 = cx.op("act", lambda e, h=h: e.activation(rden[:], banks[7][:], AF.Identity if bis == 22 else AF.Ln,
                                                                       bias=esink[:, h:h + 1]),
                                   waits=fl(td, t_es, rden_free))
                        e2 = cx.op("act", lambda e: e.activation(rden[:], rden[:], AF.Exp, scale=-1.0), waits=[e1])
                        e3 = cx.op("dve", lambda e: e.tensor_tensor(agf[:], banks[6][:], rden[:], ALU.mult),
                                   waits=fl(e2, td, agf_free))
                        e4 = cx.op("dve", lambda e, hf=hf, hb=hb: e.tensor_tensor(
                            agout[hb][:, hf * 512:(hf + 1) * 512], agf[:], sg[hb][:, hf * 512:(hf + 1) * 512],
                            ALU.mult), waits=fl(e3, sg_ready[hb], ag_free[hb]))
                        acc_free = [e3, e1]
                        rden_free = e3
                        agf_free = e4
                    q_free[hb] = last_s
                    sg_free[hb] = e4
                    dsp = cx.dma("sp", lambda e, h=h, hb=hb: e.dma_start(out=ag_sp[h], in_=agout[hb][:]), spl[hb],
                                 waits=[e4])
                    ag_free[hb] = dsp
                    spill_tix.append(dsp)

                for h in range(NQH + 1):
                    gen = attention(h - 1) if h >= 1 else None
                    if h < NQH:
                        hb = h % 2
                        chq = [(tok0, n, A[i], 0) for i, (tok0, n) in enumerate(OWN2)]
                        tq = proj_task(cx, ring, wv_in, h * 128, rhs_h, chq, banks, fl(A_free), inter=with_deferred(gen))
                        flush_deferred()
                        ea = ed = None
                        for i in range(2):
                            ea = cx.op("act", lambda e, hb=hb, i=i: e.activation(
                                qT[hb][:, i * 512:(i + 1) * 512], banks[A[i]][:, :], AF.Copy),
                                waits=fl(tq, q_free[hb]))
                            ed = cx.op("act", lambda e, i=i: e.activation(
                                q32[:, i * 512:(i + 1) * 512], banks[A[i]][:, :], AF.Copy), waits=fl(tq, rot_free))

                        def rot_hook(hb=hb, ea=ea, ed=ed):
                            nonlocal A_free, rot_free
                            tr = None
                            for i in range(2):
                                if bis == 21:
                                    break
                                tr = cx.op("pe", lambda e, i=i: e.matmul(
                                    banks[A[i]][:, :], rmat[:], q32[:, i * 512:(i + 1) * 512], start=True, stop=True),
                                    waits=fl(ea, ed, ct), signal=(i == 1))
                            ev = None
                            for i in range(2):
                                ev = cx.op("act", lambda e, i=i: e.activation(
                                    qsw[:, i * 512:(i + 1) * 512], banks[A[i]][0:32, :], AF.Copy), waits=fl(tr, rot_free))
                            o1 = cx.op("dve", lambda e: e.tensor_tensor(q32[0:32, :], q32[0:32, :], cosT[:, 128:128 + TOK],
                                                                        ALU.mult), waits=fl(tr, ed, rot_free))
                            o2 = cx.op("dve", lambda e: e.tensor_tensor(qsw[:], qsw[:], sinS[:, 128:128 + TOK], ALU.mult),
                                       waits=fl(ev, rot_free))
                            o3 = cx.op("dve", lambda e, hb=hb: e.tensor_tensor(qT[hb][0:32, :], q32[0:32, :], qsw[:], ALU.add),
                                       waits=fl(o1, o2, ea, q_free[hb]))
                            rot_free = o3
                            A_free = [ea, ed, ev]
                            q_ready[hb] = [o3, ea]
                        A_free = [ea, ed]
                        q_ready[hb] = None

                        def g_inter(gen=gen, hook=rot_hook):
                            step = 0
                            done = False
                            while True:
                                if step == 1:
                                    hook()
                                if gen is not None and not done:
                                    try:
                                        next(gen)
                                    except StopIteration:
                                        done = True
                                step += 1
                                yield
                        ginter = g_inter()
                    if h < NQH:
                        chg = [(tok0, n, B[i], 0) for i, (tok0, n) in enumerate(OWN2)]
                        tg = proj_task(cx, ring, wv_in, V_END + h * 128, rhs_h, chg, banks, fl(B_free), inter=ginter)
                        def make_silu(hb=hb, tg=tg):
                            stt = {}

                            def f_a1(i):
                                stt["a"] = cx.op("act", lambda e, i=i: e.activation(
                                    sgt[:, i * 512:(i + 1) * 512], banks[B[i]][:, :], AF.Exp, scale=-1.0),
                                    waits=fl(tg, sgt_free[0]))

                            def f_a2(i):
                                stt["a"] = cx.op("act", lambda e, i=i: e.activation(
                                    sgt[:, i * 512:(i + 1) * 512], sgt[:, i * 512:(i + 1) * 512], AF.Ln, bias=1.0),
                                    waits=[stt["a"]])

                            def f_a3(i):
                                nonlocal B_free
                                ea_ = cx.op("act", lambda e, i=i: e.activation(
                                    sgt[:, i * 512:(i + 1) * 512], sgt[:, i * 512:(i + 1) * 512], AF.Exp, scale=-1.0),
                                    waits=[stt["a"]])
                                eb_ = cx.op("dve", lambda e, hb=hb, i=i: e.tensor_tensor(
                                    sg[hb][:, i * 512:(i + 1) * 512], banks[B[i]][:, :], sgt[:, i * 512:(i + 1) * 512],
                                    ALU.mult), waits=fl(ea_, sg_free[hb]))
                                if i == 1:
                                    sgt_free[0] = eb_
                                    B_free = [ea_, eb_]
                                    sg_ready[hb] = eb_
                            out = []
                            for i in range(2):
                                out += [lambda i=i: f_a1(i), lambda i=i: f_a2(i), lambda i=i: f_a3(i)]
                            return out
                        sg_ready[hb] = None
                        deferred.extend(make_silu())
                    if h == NQH:
                        flush_deferred()
                    if gen is not None:
                        for _ in gen:
                            pass
                cx.wait_only("sp", spill_tix[-2:])
                ring.prefetch(wv_in, CB_END)
                ring.prefetch(wv_in, CC_END)
                ring.prefetch(wv_in, AG_END)
                cx.run()
        if nphase <= 2:
            return nc

        with ExitStack() as ps:
            psb = lambda name, shape, dt: ps.enter_context(nc.sbuf_tensor(name, shape, dt))
            Csb = psb("Csb", [128, 1026], F32)
            u = psb("u", [128, 1026], F32)
            c0t = psb("c0t", [128, TOK], F32)
            c1t = psb("c1t", [128, TOK], F32)
            c2t = psb("c2t", [128, TOK], F32)
            tB = psb("tB", [128, TOK], F32)
            sgc = psb("sgc", [128, TOK], F32)
            cbout = [psb("cbout%d" % i, [128, TOK], BF16) for i in range(2)]
            cx.begin()
            PA = (0, 1, 2)
            PB = (3, 4, 5)
            PA_free = None
            PB_free = None
            CH342 = [(127, 342), (469, 342), (811, 342)]
            u_free = None
            c2_free = None
            tB_free = None
            sgc_free = None
            cb_free = [None, None]
            spill_tix = []
            for c in range(KC):
                ch = [(tok0, n, PA[i], 0) for i, (tok0, n) in enumerate(CH342)]
                tC = proj_task(cx, ring, wv_in, CB_END + c * 128, rhs_h, ch, banks, fl(PA_free))
                ea = None
                for i in range(3):
                    ea = cx.op("act", lambda e, i=i: e.activation(Csb[:, i * 342:(i + 1) * 342], banks[PA[i]][:, 0:342],
                                                                   AF.Copy), waits=fl(tC, u_free))
                PA_free = [ea]
                ch = [(tok0, n, PB[i], 0) for i, (tok0, n) in enumerate(CH342)]
                tX = proj_task(cx, ring, wv_in, CC_END + c * 128, rhs_h, ch, banks, fl(PB_free))
                ed = None
                for i in range(3):
                    ed = cx.op("dve", lambda e, i=i: e.tensor_tensor(u[:, i * 342:(i + 1) * 342],
                                                                     Csb[:, i * 342:(i + 1) * 342],
                                                                     banks[PB[i]][:, 0:342], ALU.mult),
                               waits=fl(tX, ea, c2_free))
                PB_free = [ed]
                u_free = ed
                a0 = cx.op("act", lambda e, c=c: e.activation(c0t[:], u[:, 1:1025], AF.Identity,
                                                              bias=convb[:, c:c + 1],
                                                              scale=convw[:, KC + c:KC + c + 1]), waits=fl(ed, c2_free))
                v1 = cx.op("dve", lambda e, c=c: e.scalar_tensor_tensor(c1t[:], u[:, 0:1024], convw[:, c:c + 1], c0t[:],
                                                                         ALU.mult, ALU.add), waits=fl(a0, ed, c2_free))
                v2 = cx.op("dve", lambda e, c=c: e.scalar_tensor_tensor(c2t[:], u[:, 2:1026],
                                                                         convw[:, 2 * KC + c:2 * KC + c + 1], c1t[:],
                                                                         ALU.mult, ALU.add), waits=fl(v1, tB_free))
                u_free = v2
                ch = [(tok0, n, PA[i], 0) for i, (tok0, n) in enumerate(OWN2)]
                tBk = proj_task(cx, ring, wv_in, AG_END + c * 128, rhs_h, ch, banks, fl(PA_free))
                vb = None
                for i in range(2):
                    vb = cx.op("dve", lambda e, i=i: e.tensor_tensor(tB[:, i * 512:(i + 1) * 512], banks[PA[i]][:, :],
                                                                     c2t[:, i * 512:(i + 1) * 512], ALU.mult),
                               waits=fl(tBk, v2, tB_free))
                PA_free = [vb]
                c2_free = vb
                ch = [(tok0, n, PB[i], 0) for i, (tok0, n) in enumerate(OWN2)]
                tG = proj_task(cx, ring, wv_in, CX_END + c * 128, rhs_h, ch, banks, fl(PB_free))
                eg = None
                for i in range(2):
                    eg = cx.op("act", lambda e, i=i: e.activation(sgc[:, i * 512:(i + 1) * 512], banks[PB[i]][:, :],
                                                                   AF.Silu), waits=fl(tG, sgc_free))
                PB_free = [eg]
                vr = cx.op("dve", lambda e, c=c: e.tensor_tensor(cbout[c % 2][:], tB[:], sgc[:], ALU.mult),
                           waits=fl(eg, vb, cb_free[c % 2]))
                tB_free = vr
                sgc_free = vr
                dsp = cx.dma("sp", lambda e, c=c: e.dma_start(out=cb_sp[c], in_=cbout[c % 2][:]), spl[c % 2], waits=[vr])
                cb_free[c % 2] = dsp
                spill_tix.append(dsp)
            cx.wait_only("sp", spill_tix[-2:])
            ring.prefetch(wv_in, CG_END)
            ring.prefetch(wv_in, CG_END + 128)
            ring.prefetch(wv_a, 0)
            cx.run()
        if nphase <= 3:
            return nc

        PAIRS = [(0, 1), (2, 3), (4, 5), (6, 7)]
        OWNL = [(0, 512), (512, 512)]
        with ExitStack() as ps:
            psb = lambda name, shape, dt: ps.enter_context(nc.sbuf_tensor(name, shape, dt))
            actT = psb("agT", [128, KC, TOK], BF16)
            gA = [psb("gA%d" % i, [128, TOK], F32) for i in range(2)]
            mst = [psb("mst%d" % i, [128, TOK], F32) for i in range(2)]
            cx.begin()
            lds = []
            for q in range(8):
                lds.append(cx.dma("sp", lambda e, q=q: e.dma_start(
                    out=actT[:, q * 4:(q + 1) * 4, :], in_=ag_sp[q * 4:(q + 1) * 4].rearrange("j p t -> p j t")), ldq[q]))
            rhs_a = lambda kc, tok0, n: actT[:, kc, tok0:tok0 + n]
            pair_free = [None] * 4
            gA_free = [None, None]
            mst_free = [None, None]
            spill_tix = []
            np_ = [0]
            gA_ready = [None, None]

            def emit_L(j):
                ib = np_[0] % 4
                pb = PAIRS[ib]
                np_[0] += 1
                ch = [(tok0, n, pb[i], 0) for i, (tok0, n) in enumerate(OWN2)]
                tL = proj_task(cx, ring, wv_in, CG_END + j * 128, rhs_h, ch, banks, fl(pair_free[ib]))
                ea = None
                for i in range(2):
                    ea = cx.op("act", lambda e, i=i, j=j, pb=pb: e.activation(
                        gA[j % 2][:, i * 512:(i + 1) * 512], banks[pb[i]][:, :], AF.Sigmoid, bias=bm[:, j:j + 1]),
                        waits=fl(tL, gA_free[j % 2]))
                pair_free[ib] = [ea]
                gA_ready[j % 2] = ea

            def emit_Y(j):
                ia = np_[0] % 4
                pa = PAIRS[ia]
                np_[0] += 1
                ch = [(tok0, n, pa[i], 0) for i, (tok0, n) in enumerate(OWNL)]
                tY = proj_task(cx, ring, wv_a, j * 128, rhs_a, ch, banks, fl(pair_free[ia]),
                               kc_waits=({q * 4: [lds[q]] for q in range(8)} if j == 0 else None))
                vm = None
                for i in range(2):
                    vm = cx.op("dve", lambda e, i=i, j=j, pa=pa: e.tensor_tensor(
                        mst[j % 2][:, i * 512:(i + 1) * 512], banks[pa[i]][:, :], gA[j % 2][:, i * 512:(i + 1) * 512],
                        ALU.mult), waits=fl(tY, gA_ready[j % 2], mst_free[j % 2]))
                pair_free[ia] = [vm]
                gA_free[j % 2] = vm
                dsp = cx.dma("sp", lambda e, j=j: e.dma_start(out=ma_sp[j], in_=mst[j % 2][:]), spl[j % 2], waits=[vm])
                mst_free[j % 2] = dsp
                spill_tix.append(dsp)

            emit_L(0)
            emit_L(1)
            for j in range(KC):
                emit_Y(j)
                if j + 2 < KC:
                    emit_L(j + 2)
            cx.wait_only("sp", spill_tix[-2:])
            ring.prefetch(wv_in, MA_END)
            ring.prefetch(wv_in, MA_END + 128)
            ring.prefetch(wv_b, 0)
            cx.run()
        if nphase <= 4:
            return nc

        with ExitStack() as ps:
            psb = lambda name, shape, dt: ps.enter_context(nc.sbuf_tensor(name, shape, dt))
            actT = psb("cbT", [128, KC, TOK], BF16)
            gB = [psb("gB%d" % i, [128, TOK], F32) for i in range(2)]
            mAin = [psb("mAin%d" % i, [128, TOK], F32) for i in range(2)]
            tYb = psb("tYb", [128, TOK], F32)
            mout = [psb("mout%d" % i, [128, TOK], BF16) for i in range(2)]
            cx.begin()
            lds = []
            for q in range(8):
                lds.append(cx.dma("sp", lambda e, q=q: e.dma_start(
                    out=actT[:, q * 4:(q + 1) * 4, :], in_=cb_sp[q * 4:(q + 1) * 4].rearrange("j p t -> p j t")), ldq[q]))
            rhs_a = lambda kc, tok0, n: actT[:, kc, tok0:tok0 + n]
            pair_free = [None] * 4
            gB_free = [None, None]
            mAin_free = [None, None]
            mout_free = [None, None]
            tYb_free = None
            msem = [dB, dC]
            spill_tix = []
            np_ = [0]
            gB_ready = [None, None]

            def emit_L(j):
                ib = np_[0] % 4
                pb = PAIRS[ib]
                np_[0] += 1
                ch = [(tok0, n, pb[i], 0) for i, (tok0, n) in enumerate(OWN2)]
                tL = proj_task(cx, ring, wv_in, MA_END + j * 128, rhs_h, ch, banks, fl(pair_free[ib]))
                ea = None
                for i in range(2):
                    ea = cx.op("act", lambda e, i=i, j=j, pb=pb: e.activation(
                        gB[j % 2][:, i * 512:(i + 1) * 512], banks[pb[i]][:, :], AF.Sigmoid,
                        bias=bm[:, 32 + j:33 + j]), waits=fl(tL, gB_free[j % 2]))
                pair_free[ib] = [ea]
                gB_ready[j % 2] = ea

            def emit_Y(j):
                nonlocal tYb_free
                ia = np_[0] % 4
                pa = PAIRS[ia]
                np_[0] += 1
                lm = cx.dma("sp", lambda e, j=j: e.dma_start(out=mAin[j % 2][:], in_=ma_sp[j]), msem[j % 2],
                            waits=fl(mAin_free[j % 2]))
                ch = [(tok0, n, pa[i], 0) for i, (tok0, n) in enumerate(OWNL)]
                tY = proj_task(cx, ring, wv_b, j * 128, rhs_a, ch, banks, fl(pair_free[ia]),
                               kc_waits=({q * 4: [lds[q]] for q in range(8)} if j == 0 else None))
                vm = None
                for i in range(2):
                    vm = cx.op("dve", lambda e, i=i, j=j, pa=pa: e.tensor_tensor(
                        tYb[:, i * 512:(i + 1) * 512], banks[pa[i]][:, :], gB[j % 2][:, i * 512:(i + 1) * 512],
                        ALU.mult), waits=fl(tY, gB_ready[j % 2], tYb_free))
                pair_free[ia] = [vm]
                gB_free[j % 2] = vm
                va = cx.op("dve", lambda e, j=j: e.tensor_tensor(mout[j % 2][:], tYb[:], mAin[j % 2][:], ALU.add),
                           waits=fl(vm, lm, mout_free[j % 2]))
                tYb_free = va
                mAin_free[j % 2] = va
                dsp = cx.dma("sp", lambda e, j=j: e.dma_start(out=m_sp[j], in_=mout[j % 2][:]), spl[j % 2], waits=[va])
                mout_free[j % 2] = dsp
                spill_tix.append(dsp)

            emit_L(0)
            emit_L(1)
            for j in range(KC):
                emit_Y(j)
                if j + 2 < KC:
                    emit_L(j + 2)
            cx.wait_only("sp", spill_tix[-2:])
            for j_ in range(3):
                ring.prefetch(wv_o, j_ * 128)
            cx.run()
        if nphase <= 5:
            return nc
        hs.close()
        xp = [es.enter_context(nc.sbuf_tensor("xp%d" % i, [128, D], F32)) for i in range(4)]
        xps = [cx.new_sem("xps%d" % i) for i in range(4)]
        xpt = [None] * 4

        with ExitStack() as ps:
            psb = lambda name, shape, dt: ps.enter_context(nc.sbuf_tensor(name, shape, dt))
            actT = psb("mT", [128, KC, TOK], BF16)
            oT_sb = [psb("oTsb%d" % i, [128, TOK], F32) for i in range(2)]
            ostage = [psb("ostage%d" % i, [128, 8, 128], F32) for i in range(2)]
            junk6 = psb("junk6", [128, 128], F32)
            gpost = psb("gpost_s", [128, D], F32)
            cx.begin()
            tgp = cx.dma("sp", lambda e: e.dma_start(out=gpost[:], in_=gpost_d), csem)
            lds = []
            for q in range(8):
                lds.append(cx.dma("sp", lambda e, q=q: e.dma_start(
                    out=actT[:, q * 4:(q + 1) * 4, :], in_=m_sp[q * 4:(q + 1) * 4].rearrange("j p t -> p j t")), ldq[q]))
            rhs_a = lambda kc, tok0, n: actT[:, kc, tok0:tok0 + n]
            out_v = out_d.rearrange("(tt p) d -> p tt d", p=128)
            pair_free = [None, None]
            j6_free = [None]
            oT_free = [None, None]
            tp_free = [None, None]
            ost_free = [None, None]
            pending = None
            spill_tix = []
            for j in range(KC + 1):
                if j < KC:
                    jb = j % 2
                    pa = PAIRS[jb]
                    ch = [(tok0, n, pa[i], 0) for i, (tok0, n) in enumerate(OWNL)]
                    tO = proj_task(cx, ring, wv_o, j * 128, rhs_a, ch, banks, fl(pair_free[jb]),
                                   kc_waits=({q * 4: [lds[q]] for q in range(8)} if j == 0 else None))
                if pending is not None:
                    pending()
                    pending = None
                if j == 8:
                    for i_ in range(4):
                        xpt[i_] = cx.dma("sp", lambda e, i_=i_: e.dma_start(
                            out=xp[i_][:], in_=x_ext[(i_ + 1) * 128:(i_ + 2) * 128, :]), xps[i_])
                if j < KC:
                    ea = None
                    for i in range(2):
                        ea = cx.op("act", lambda e, i=i, jb=jb, pa=pa: e.activation(
                            oT_sb[jb][:, i * 512:(i + 1) * 512], banks[pa[i]][:, :], AF.Copy),
                            waits=fl(tO, oT_free[jb]))
                    pair_free[jb] = [ea]

                    def post(j=j, jb=jb, ea=ea):
                        tpb = (4 + 2 * jb, 5 + 2 * jb)
                        tpt = None
                        for tt in range(8):
                            dst = banks[tpb[tt // 4]][:, (tt % 4) * 128:(tt % 4 + 1) * 128]
                            tpt = cx.op("pe", lambda e, dst=dst, tt=tt, jb=jb: e.transpose(
                                dst, oT_sb[jb][:, tt * 128:(tt + 1) * 128], identf[:]),
                                waits=fl(ea, tp_free[jb]) if tt == 0 else (), signal=(tt == 7))
                        oT_free[jb] = tpt
                        sq = None
                        for tt in range(8):
                            src = banks[tpb[tt // 4]][:, (tt % 4) * 128:(tt % 4 + 1) * 128]
                            sq = cx.op("act", lambda e, src=src, tt=tt, j=j: e.activation(
                                junk6[:], src, AF.Square, accum_out=ssq[:, tt * KC + j:tt * KC + j + 1]),
                                waits=fl(tpt, j6_free[0]))
                            j6_free[0] = sq
                        cp = None
                        for tt in range(8):
                            src = banks[tpb[tt // 4]][:, (tt % 4) * 128:(tt % 4 + 1) * 128]
                            cp = cx.op("dve", lambda e, src=src, tt=tt, j=j, jb=jb: e.tensor_tensor(
                                ostage[jb][:, tt, :], src, gpost[:, j * 128:(j + 1) * 128], ALU.mult),
                                waits=fl(sq, tgp, ost_free[jb]))
                        tp_free[jb] = [cp]
                        dsp = cx.dma("sp", lambda e, j=j, jb=jb: e.dma_start(
                            out=out_v[:, :, j * 128:(j + 1) * 128], in_=ostage[jb][:]), spl[jb], waits=[cp])
                        ost_free[jb] = [dsp]
                        spill_tix.append(dsp)
                    pending = post
            cx.wait_only("sp", spill_tix[-2:])
            cx.run()
        if nphase <= 6:
            return nc

        with ExitStack() as ps:
            psb = lambda name, shape, dt: ps.enter_context(nc.sbuf_tensor(name, shape, dt))
            ot = [psb("ot%d" % i, [128, D], F32) for i in range(2)]
            xr = [psb("xr%d" % i, [128, D], F32) for i in range(2)]
            rs = psb("rs", [128, 24], F32)
            cx.begin()
            r1 = cx.op("dve", lambda e: e.reduce_sum(rs[:, 0:8], ssq[:].rearrange("p (t j) -> p t j", j=KC),
                                                     mybir.AxisListType.X))
            r2 = cx.op("dve", lambda e: e.tensor_scalar(rs[:, 8:16], rs[:, 0:8], 1.0 / D, EPS, ALU.mult, ALU.add),
                       waits=[r1])
            r3 = cx.op("act", lambda e: e.activation(rs[:, 16:24], rs[:, 8:16], AF.Sqrt), waits=[r2])
            r4 = cx.op("dve", lambda e: e.reciprocal(rstd_o[:], rs[:, 16:24]), waits=[r3])
            lsem = [dA, dB]
            ssem = [dC, dD]
            buf_free = [None, None]
            stores = []
            for tt in range(8):
                b = tt % 2
                l1 = cx.dma("sp", lambda e, tt=tt, b=b: e.dma_start(out=ot[b][:], in_=out_d[tt * 128:(tt + 1) * 128, :]),
                            lsem[b], waits=fl(buf_free[b]))
                if tt < 4:
                    l2 = xpt[tt]
                    xsrc = xp[tt]
                else:
                    l2 = cx.dma("sp", lambda e, tt=tt, b=b: e.dma_start(
                        out=xr[b][:], in_=x_ext[(tt + 1) * 128:(tt + 2) * 128, :]), lsem[b], waits=fl(buf_free[b]))
                    xsrc = xr[b]
                v1 = cx.op("dve", lambda e, tt=tt, b=b, xsrc=xsrc: e.scalar_tensor_tensor(
                    ot[b][:], ot[b][:], rstd_o[:, tt:tt + 1], xsrc[:], ALU.mult, ALU.add), waits=fl(l1, l2, r4))
                s1 = cx.dma("sp", lambda e, tt=tt, b=b: e.dma_start(out=out_d[tt * 128:(tt + 1) * 128, :], in_=ot[b][:]),
                            ssem[b], waits=[v1])
                buf_free[b] = s1
                stores.append(s1)
            cx.wait_only("sp", stores[-2:])
            cx.run()
        return nc


def _host_prep(x, norm_pre, w_in, b_merge, attn_sink, conv_w, conv_b, w_attn_out, w_conv_out, w_out, norm_post):
    f32 = np.float32
    x2 = np.asarray(x, f32).reshape(S, D)
    w_in2 = np.ascontiguousarray(np.asarray(w_in, f32).reshape(D, IN_COLS))
    w_a2 = np.ascontiguousarray(np.asarray(w_attn_out, f32).reshape(D, D))
    w_b2 = np.ascontiguousarray(np.asarray(w_conv_out, f32).reshape(D, D))
    w_o2 = np.ascontiguousarray(np.asarray(w_out, f32).reshape(D, D))
    gpre = np.ascontiguousarray(np.asarray(norm_pre, f32).reshape(KC, 128).T)
    bm = np.ascontiguousarray(np.asarray(b_merge, f32).reshape(64, 128).T)
    sinkb = np.ascontiguousarray(np.broadcast_to(np.asarray(attn_sink, f32).reshape(1, NQH), (128, NQH)))
    cw = np.asarray(conv_w, f32).reshape(3, KC, 128)
    convw = np.ascontiguousarray(cw.transpose(2, 0, 1).reshape(128, 3 * KC))
    convb = np.ascontiguousarray(np.asarray(conv_b, f32).reshape(KC, 128).T)
    gpost = np.ascontiguousarray(np.broadcast_to(np.asarray(norm_post, f32).reshape(1, D), (128, D)))
    gprebc = np.ascontiguousarray(np.broadcast_to(np.asarray(norm_pre, f32).reshape(1, D), (128, D)))
    identf = np.eye(128, dtype=f32)
    identb = np.eye(128, dtype=f32).astype(ml_dtypes.bfloat16)
    onesb = np.ones((128, 128), dtype=f32).astype(ml_dtypes.bfloat16)
    jj = np.arange(128)[:, None]
    ii = np.arange(128)[None, :]
    NEG = np.float32(-30000.0)
    mL = np.where(jj >= ii, np.float32(0.0), NEG).astype(f32)
    mR = np.where(jj <= ii, np.float32(0.0), NEG).astype(f32)
    zero = np.full((128, 128), NEG, f32)
    rmat = np.zeros((128, 128), f32)
    for p in range(16):
        rmat[p + 16, p] = 1.0
        rmat[p, p + 16] = 1.0
    rot = 32
    inv_freq = (np.float32(500000.0) ** (-np.arange(0, rot, 2, dtype=f32) / np.float32(rot))).astype(f32)
    xpad = np.zeros((S + 256, D), f32)
    xpad[128:128 + S] = x2
    in_maps = []
    for c in range(NCORES):
        pos = (np.arange(EXT, dtype=np.int64) + c * TOK - 128)
        posf = np.clip(pos, 0, S - 1).astype(f32)
        ang = (posf[:, None] * inv_freq[None, :]).astype(f32)
        cosv = np.cos(ang).astype(f32).T
        sinv = np.sin(ang).astype(f32).T
        cosT = np.ascontiguousarray(np.concatenate([cosv, cosv], axis=0))
        sinS = np.ascontiguousarray(np.concatenate([-sinv, sinv], axis=0))
        masks = np.concatenate([zero if c == 0 else mL, mL, mR, zero if c == NCORES - 1 else mR], axis=1)
        in_maps.append({
            "x_ext": np.ascontiguousarray(xpad[c * TOK:c * TOK + EXT]),
            "w_in": w_in2, "w_a": w_a2, "w_b": w_b2, "w_o": w_o2,
            "gpre": gpre, "bm": bm, "sinkb": sinkb, "convw": convw, "convb": convb, "gpost": gpost, "gprebc": gprebc,
            "identf": identf, "identb": identb, "onesb": onesb,
            "masks": np.ascontiguousarray(masks).astype(ml_dtypes.bfloat16),
            "cosT": cosT, "sinT": sinS, "rmat": rmat,
        })
    return in_maps


def kernel(x, norm_pre, w_in, b_merge, attn_sink, conv_w, conv_b, w_attn_out, w_conv_out, w_out, norm_post):
    in_maps = _host_prep(x, norm_pre, w_in, b_merge, attn_sink, conv_w, conv_b, w_attn_out, w_conv_out, w_out,
                         norm_post)
    nc = build_nc()
    res = run_bass_kernel_spmd(nc, in_maps, core_ids=list(range(NCORES)))
    outs = [np.asarray(r["out"], dtype=np.float32) for r in res.results]
    return np.concatenate(outs, axis=0).reshape(1, S, D)
```

```python
import numpy as np
import ml_dtypes
from contextlib import ExitStack

import concourse.bass as bass
import concourse.mybir as mybir
from concourse.bass_utils import run_bass_kernel_spmd

F32 = mybir.dt.float32
BF16 = mybir.dt.bfloat16
AF = mybir.ActivationFunctionType
ALU = mybir.AluOpType

D = 4096
S = 8192
NCORES = 8
TOK = S // NCORES
EXT = TOK + 256
KC = D // 128
HD = 128
NQH = 32
NKVH = 8
Q_END = 4096
K_END = Q_END + 1024
V_END = K_END + 1024
AG_END = V_END + 4096
CB_END = AG_END + 4096
CC_END = CB_END + 4096
CX_END = CC_END + 4096
CG_END = CX_END + 4096
MA_END = CG_END + 4096
IN_COLS = MA_END + 4096
EPS = 1e-6
SCALE = HD ** -0.5
NSLOT = 4

ENGS = ("pe", "act", "dve", "pool", "sp")


class Sem:
    def __init__(self, h):
        self.h = h
        self.count = 0


class Ctx:
    def __init__(self, nc, es):
        self.nc = nc
        self.es = es
        self.esem = {e: Sem(es.enter_context(nc.semaphore("sq_" + e))) for e in ("pe", "act", "dve", "pool")}
        self.waited = {e: {} for e in ENGS}
        self.ops = None
        self.nsem = 4

    def new_sem(self, name):
        self.nsem += 1
        return Sem(self.es.enter_context(self.nc.semaphore(name)))

    def begin(self):
        self.ops = {e: [] for e in ENGS}

    def _waits(self, eng, waits):
        w = []
        best = {}
        for t in waits:
            if t is None:
                continue
            s, v = t
            if id(s) not in best or best[id(s)][1] < v:
                best[id(s)] = (s, v)
        for (s, v) in best.values():
            if self.waited[eng].get(id(s), 0) >= v:
                continue
            self.waited[eng][id(s)] = v
            w.append((s.h, v))
        return w

    def op(self, eng, fn, waits=(), signal=True):
        w = self._waits(eng, waits)
        inc = None
        ticket = None
        if signal:
            s = self.esem[eng]
            s.count += 1
            inc = (s.h, 1)
            ticket = (s, s.count)
        self.ops[eng].append((fn, w, inc))
        return ticket

    def dma(self, eng, fn, dsem, waits=()):
        w = self._waits(eng, waits)
        dsem.count += 16
        self.ops[eng].append((fn, w, (dsem.h, 16)))
        return (dsem, dsem.count)

    def wait_only(self, eng, waits):
        w = self._waits(eng, waits)
        if w:
            self.ops[eng].append((None, w, None))

    def run(self):
        nc = self.nc
        ops = self.ops

        def replay(e, lst):
            for fn, w, inc in lst:
                for (h, v) in w:
                    e.wait_ge(h, v)
                if fn is None:
                    continue
                ins = fn(e)
                if inc is not None:
                    ins.then_inc(inc[0], inc[1])

        with nc.Block() as block:
            @block.tensor
            def _(e):
                replay(e, ops["pe"])

            @block.scalar
            def _(e):
                replay(e, ops["act"])

            @block.vector
            def _(e):
                replay(e, ops["dve"])

            @block.gpsimd
            def _(e):
                replay(e, ops["pool"])

            @block.sync
            def _(e):
                replay(e, ops["sp"])
        self.ops = None


class WRing:
    def __init__(self, cx, slots):
        self.cx = cx
        self.slots = slots
        self.lds = [cx.new_sem("wld%d" % i) for i in range(len(slots))]
        self.free = [None] * len(slots)
        self.n = 0
        self.pref = []

    def prefetch(self, wview, c0):
        i, t = self._load(wview, c0)
        self.pref.append(((id(wview), c0), i, t))

    def load(self, wview, c0):
        if self.pref:
            key, i, t = self.pref.pop(0)
            assert key == (id(wview), c0), "ring prefetch order mismatch"
            return i, t
        return self._load(wview, c0)

    def _load(self, wview, c0):
        i = self.n % len(self.slots)
        self.n += 1
        slot = self.slots[i]
        c0 = c0 % wview.shape[2]
        src = wview[:, :, c0:c0 + 128]
        t = self.cx.dma("pool", lambda e, slot=slot, src=src: e.dma_start(out=slot[:], in_=src),
                        self.lds[i], waits=[self.free[i]])
        return i, t

    def release(self, i, ticket):
        self.free[i] = ticket


def proj_task(cx, ring, wview, c0, rhs_fn, chunks, banks, bank_free, inter=None, every=4, kc_waits=None):
    si, lt = ring.load(wview, c0)
    slot = ring.slots[si]
    first = True
    ticket = None
    nmm = KC * len(chunks)
    k = 0
    for kc in range(KC):
        for ci, (tok0, n, bi, col0) in enumerate(chunks):
            k += 1
            last = (k == nmm)
            out = banks[bi][:, col0:col0 + n]
            lhsT = slot[:, kc, :]
            rhs = rhs_fn(kc, tok0, n)
            waits = ([lt] + list(bank_free)) if first else []
            if kc_waits is not None and ci == 0 and kc in kc_waits:
                waits = list(waits) + list(kc_waits[kc])
            first = False
            fn = (lambda e, out=out, lhsT=lhsT, rhs=rhs, st=(kc == 0), sp=(kc == KC - 1):
                  e.matmul(out, lhsT, rhs, start=st, stop=sp))
            t = cx.op("pe", fn, waits=waits, signal=last)
            if last:
                ticket = t
        if inter is not None and kc % every == every - 1 and kc != KC - 1:
            next(inter, None)
    ring.release(si, ticket)
    return ticket


def build_nc(debug=False, stop_after=None, small_w=False, bis=99):
    nc = bass.Bass("TRN2", target_bir_lowering=False)
    dk = "ExternalOutput" if debug else "Internal"
    x_ext = nc.dram_tensor("x_ext", [EXT, D], F32, kind="ExternalInput").ap()
    w_in = nc.dram_tensor("w_in", [D, IN_COLS] if not small_w else [D, 128], F32, kind="ExternalInput").ap()
    w_a = nc.dram_tensor("w_a", [D, D] if not small_w else [D, 128], F32, kind="ExternalInput").ap()
    w_b = nc.dram_tensor("w_b", [D, D] if not small_w else [D, 128], F32, kind="ExternalInput").ap()
    w_o = nc.dram_tensor("w_o", [D, D] if not small_w else [D, 128], F32, kind="ExternalInput").ap()
    gpre_d = nc.dram_tensor("gpre", [128, KC], F32, kind="ExternalInput").ap()
    bm_d = nc.dram_tensor("bm", [128, 64], F32, kind="ExternalInput").ap()
    sink_d = nc.dram_tensor("sinkb", [128, NQH], F32, kind="ExternalInput").ap()
    convw_d = nc.dram_tensor("convw", [128, 3 * KC], F32, kind="ExternalInput").ap()
    convb_d = nc.dram_tensor("convb", [128, KC], F32, kind="ExternalInput").ap()
    gpost_d = nc.dram_tensor("gpost", [128, D], F32, kind="ExternalInput").ap()
    gprebc_d = nc.dram_tensor("gprebc", [128, D], F32, kind="ExternalInput").ap()
    identf_d = nc.dram_tensor("identf", [128, 128], F32, kind="ExternalInput").ap()
    identb_d = nc.dram_tensor("identb", [128, 128], BF16, kind="ExternalInput").ap()
    onesb_d = nc.dram_tensor("onesb", [128, 128], BF16, kind="ExternalInput").ap()
    masks_d = nc.dram_tensor("masks", [128, 4 * 128], BF16, kind="ExternalInput").ap()
    rm_d = nc.dram_tensor("rmat", [128, 128], F32, kind="ExternalInput").ap()
    cos_d = nc.dram_tensor("cosT", [32, EXT], F32, kind="ExternalInput").ap()
    sin_d = nc.dram_tensor("sinT", [32, EXT], F32, kind="ExternalInput").ap()
    out_d = nc.dram_tensor("out", [TOK, D], F32, kind="ExternalOutput").ap()
    ag_sp = nc.dram_tensor("ag_sp", [KC, 128, TOK], BF16, kind=dk).ap()
    cb_sp = nc.dram_tensor("cb_sp", [KC, 128, TOK], BF16, kind=dk).ap()
    ma_sp = nc.dram_tensor("ma_sp", [KC, 128, TOK], F32, kind=dk).ap()
    m_sp = nc.dram_tensor("m_sp", [KC, 128, TOK], BF16, kind=dk).ap()
    dbg = {}
    if debug:
        dbg["hT"] = nc.dram_tensor("dbg_hT", [128, KC * EXT], BF16, kind="ExternalOutput").ap()
        dbg["kT"] = nc.dram_tensor("dbg_kT", [128, NKVH * EXT], BF16, kind="ExternalOutput").ap()
        dbg["v"] = nc.dram_tensor("dbg_v", [128, 10 * 1024], BF16, kind="ExternalOutput").ap()

    wv_in = w_in.rearrange("(kc p) c -> p kc c", p=128)
    wv_a = w_a.rearrange("(kc p) c -> p kc c", p=128)
    wv_b = w_b.rearrange("(kc p) c -> p kc c", p=128)
    wv_o = w_o.rearrange("(kc p) c -> p kc c", p=128)

    with ExitStack() as es:
        cx = Ctx(nc, es)
        sb = lambda name, shape, dt: es.enter_context(nc.sbuf_tensor(name, shape, dt))
        wslots = [sb("wr%d" % i, [128, KC, 128], BF16) for i in range(NSLOT)]
        gpre = sb("gpre_s", [128, KC], F32)
        bm = sb("bm_s", [128, 64], F32)
        sinkb = sb("sink_s", [128, NQH], F32)
        esink = sb("esink_s", [128, NQH], F32)
        convw = sb("convw_s", [128, 3 * KC], F32)
        convb = sb("convb_s", [128, KC], F32)
        identf = sb("identf_s", [128, 128], F32)
        identb = sb("identb_s", [128, 128], BF16)
        onesb = sb("onesb_s", [128, 128], BF16)
        masks = sb("masks_s", [128, 512], BF16)
        rmat = sb("rmat_s", [128, 128], F32)
        ssq = sb("ssq_s", [128, 8 * KC], F32)
        rstd_o = sb("rstd_o", [128, 8], F32)
        banks = [es.enter_context(nc.psum_tensor("bank%d" % i, [128, 512], F32)) for i in range(8)]
        ring = WRing(cx, wslots)
        csem = cx.new_sem("const")
        dA = cx.new_sem("dA")
        dB = cx.new_sem("dB")
        dC = cx.new_sem("dC")
        dD = cx.new_sem("dD")
        ldq = [cx.new_sem("ldq%d" % i) for i in range(8)]
        spl = [cx.new_sem("spl%d" % i) for i in range(2)]
        hs = es.enter_context(ExitStack())
        hT = hs.enter_context(nc.sbuf_tensor("hT", [128, KC, EXT], BF16))

        def fl(*ts):
            o = []
            for t in ts:
                if t is None:
                    continue
                if isinstance(t, list):
                    o.extend(fl(*t))
                else:
                    o.append(t)
            return o

        OWN2 = [(128, 512), (640, 512)]
        rhs_h = lambda kc, tok0, n: hT[:, kc, tok0:tok0 + n]
        nphase = stop_after if stop_after is not None else 99

        with ExitStack() as ps:
            psb = lambda name, shape, dt: ps.enter_context(nc.sbuf_tensor(name, shape, dt))
            xt = [psb("xt%d" % i, [128, D], F32) for i in range(2)]
            hnb = [psb("hnb%d" % i, [128, D], BF16) for i in range(2)]
            junk = psb("junk0", [128, D], BF16)
            gbc = psb("gbc", [128, D], F32)
            st = psb("st0", [128, 40], F32)
            cx.begin()
            consts = [(gpre, gpre_d), (bm, bm_d), (sinkb, sink_d), (convw, convw_d), (convb, convb_d),
                      (identf, identf_d), (identb, identb_d), (onesb, onesb_d), (masks, masks_d), (rmat, rm_d)]
            ct = None
            for (s_t, d_t) in consts:
                ct = cx.dma("sp", lambda e, s_t=s_t, d_t=d_t: e.dma_start(out=s_t[:], in_=d_t), csem)
            tgb = cx.dma("sp", lambda e: e.dma_start(out=gbc[:], in_=gprebc_d), dC)
            xsem = [dA, dB]
            xt_free = [None, None]
            hn_free = [None, None]
            bank_free = [None] * 8
            bbf = [banks[i][:].bitcast(BF16) for i in range(8)]
            junk_free = None
            for t in range(10):
                b = t % 2
                lt = cx.dma("sp", lambda e, b=b, t=t: e.dma_start(out=xt[b][:], in_=x_ext[t * 128:(t + 1) * 128, :]),
                            xsem[b], waits=fl(xt_free[b]))
                t_sq = cx.op("act", lambda e, b=b, t=t: e.activation(junk[:], xt[b][:], AF.Square,
                                                                      accum_out=st[:, t:t + 1]), waits=fl(lt, junk_free))
                junk_free = t_sq
                t1 = cx.op("dve", lambda e, t=t: e.tensor_scalar(st[:, 10 + t:11 + t], st[:, t:t + 1], 1.0 / D, EPS,
                                                                  ALU.mult, ALU.add), waits=[t_sq])
                t2 = cx.op("act", lambda e, t=t: e.activation(st[:, 20 + t:21 + t], st[:, 10 + t:11 + t], AF.Sqrt),
                           waits=[t1])
                t3 = cx.op("dve", lambda e, t=t: e.reciprocal(st[:, 30 + t:31 + t], st[:, 20 + t:21 + t]), waits=[t2])
                t4 = cx.op("dve", lambda e, b=b, t=t: e.scalar_tensor_tensor(
                    hnb[b][:], xt[b][:], st[:, 30 + t:31 + t], gbc[:], ALU.mult, ALU.mult),
                    waits=fl(t3, lt, tgb, hn_free[b]))
                xt_free[b] = [t4, t_sq]
                tp = None
                for g in range(4):
                    bi = b * 4 + g
                    for i in range(8):
                        kc = g * 8 + i
                        w = fl(t4, ct, bank_free[bi]) if i == 0 else ()
                        tp = cx.op("pe", lambda e, bi=bi, i=i, b=b, kc=kc: e.transpose(
                            bbf[bi][:, i * 128:(i + 1) * 128], hnb[b][:, kc * 128:(kc + 1) * 128], identb[:]),
                            waits=w, signal=(i == 7))
                    src = bbf[bi][:, :].rearrange("p (k d) -> p k d", d=128)
                    dst = hT[:, g * 8:(g + 1) * 8, t * 128:(t + 1) * 128]
                    if g % 2 == 0:
                        ev = cx.op("act", lambda e, src=src, dst=dst: e.activation(dst, src, AF.Copy), waits=[tp])
                    else:
                        ev = cx.op("dve", lambda e, src=src, dst=dst: e.tensor_copy(dst, src), waits=[tp])
                    bank_free[bi] = [ev]
                hn_free[b] = tp
            cx.wait_only("sp", fl(xt_free, ct))
            cx.wait_only("pe", fl(bank_free))
            if debug:
                dt_ = cx.dma("sp", lambda e: e.dma_start(out=dbg["hT"], in_=hT[:].rearrange("p k t -> p (k t)")),
                             dC, waits=fl(bank_free))
                cx.wait_only("sp", [dt_])
            for g_ in range(3):
                ring.prefetch(wv_in, Q_END + g_ * 128)
            cx.run()
        if nphase <= 0:
            return nc

        with ExitStack() as att:
            asb = lambda name, shape, dt: att.enter_context(nc.sbuf_tensor(name, shape, dt))
            kT = asb("kT", [128, NKVH, EXT], BF16)
            vtok = asb("vtok", [128, 10, 1024], BF16)
            cosT = asb("cosT_s", [32, EXT], F32)
            sinS = asb("sinS_s", [32, EXT], F32)
            with ExitStack() as ps:
                psb = lambda name, shape, dt: ps.enter_context(nc.sbuf_tensor(name, shape, dt))
                k32 = psb("k32", [32, EXT], F32)
                ksw = psb("ksw", [32, EXT], F32)
                kt1 = psb("kt1", [32, EXT], F32)
                kt2 = psb("kt2", [32, EXT], F32)
                vT_sb = [psb("vTsb%d" % i, [128, EXT], BF16) for i in range(2)]
                cx.begin()
                tc1 = cx.dma("sp", lambda e: e.dma_start(out=cosT[:], in_=cos_d), csem)
                tc2 = cx.dma("sp", lambda e: e.dma_start(out=sinS[:], in_=sin_d), csem)
                sets = [(0, 1, 2), (3, 4, 5)]
                set_free = [None, None]
                CH3 = [(0, 512), (512, 512), (1024, 256)]
                pending = None
                k_post = None
                tp_free = None
                vsb_free = [None, None]
                b6 = banks[6][:].bitcast(BF16)
                b7 = banks[7][:].bitcast(BF16)
                last = []
                for ti in range(16):
                    isk = ti < 8
                    g = ti % 8
                    bs = sets[ti % 2]
                    c0 = (Q_END if isk else K_END) + g * 128
                    chunks = [(tok0, n, bs[i], 0) for i, (tok0, n) in enumerate(CH3)]
                    tk = proj_task(cx, ring, wv_in, c0, rhs_h, chunks, banks, fl(set_free[ti % 2]))
                    if pending is not None:
                        pending()
                        pending = None
                    if isk:
                        ea = ed = None
                        for i, (tok0, n) in enumerate(CH3):
                            ea = cx.op("act", lambda e, g=g, i=i, tok0=tok0, n=n, bs=bs: e.activation(
                                kT[:, g, tok0:tok0 + n], banks[bs[i]][:, 0:n], AF.Copy), waits=[tk])
                            if bis >= 11:
                                ed = cx.op("act", lambda e, i=i, tok0=tok0, n=n, bs=bs: e.activation(
                                    k32[0:32, tok0:tok0 + n], banks[bs[i]][0:32, 0:n], AF.Copy),
                                    waits=fl(tk, k_post))
                        set_free[ti % 2] = [ea, ed]
                        if bis < 12 or bis == 15:
                            last = [ea, ed]
                            continue
                        d1 = cx.dma("sp", lambda e: e.dma_start(out=ksw[0:16, :], in_=k32[16:32, :]), dC,
                                    waits=fl(ed, k_post))
                        d2 = cx.dma("sp", lambda e: e.dma_start(out=ksw[16:32, :], in_=k32[0:16, :]), dC,
                                    waits=fl(ed, k_post))
                        o1 = cx.op("dve", lambda e: e.tensor_tensor(kt1[:], k32[:], cosT[:], ALU.mult),
                                   waits=fl(ed, tc1, tc2, k_post))
                        o2 = cx.op("dve", lambda e: e.tensor_tensor(kt2[:], ksw[:], sinS[:], ALU.mult),
                                   waits=fl(d1, d2, tc2, k_post))
                        o3 = cx.op("dve", lambda e, g=g: e.tensor_tensor(kT[0:32, g, :], kt1[:], kt2[:], ALU.add),
                                   waits=[o1, o2, ea])
                        k_post = o3
                        last = [o3, ea]
                    else:
                        vb = ti % 2
                        ea = None
                        for i, (tok0, n) in enumerate(CH3):
                            ea = cx.op("act", lambda e, vb=vb, i=i, tok0=tok0, n=n, bs=bs: e.activation(
                                vT_sb[vb][:, tok0:tok0 + n], banks[bs[i]][:, 0:n], AF.Copy),
                                waits=fl(tk, vsb_free[vb]))
                        set_free[ti % 2] = [ea]
                        if bis < 13:
                            last = [ea]
                            continue

                        def post(g=g, vb=vb, ea=ea):
                            nonlocal tp_free, last
                            tpt = None
                            for t in range(10):
                                dst = (b6 if t < 5 else b7)[:, (t % 5) * 128:(t % 5 + 1) * 128]
                                tpt = cx.op("pe", lambda e, dst=dst, t=t, vb=vb: e.transpose(
                                    dst, vT_sb[vb][:, t * 128:(t + 1) * 128], identb[:]),
                                    waits=fl(ea, tp_free) if t == 0 else (), signal=(t == 9))
                            vsb_free[vb] = tpt
                            if bis < 14:
                                tp_free = tpt
                                last = [tpt]
                                return
                            c1 = cx.op("dve", lambda e, g=g: e.tensor_copy(
                                vtok[:, 0:5, g * 128:(g + 1) * 128],
                                b6[:, 0:640].rearrange("p (t d) -> p t d", d=128)), waits=[tpt])
                            c2 = cx.op("dve", lambda e, g=g: e.tensor_copy(
                                vtok[:, 5:10, g * 128:(g + 1) * 128],
                                b7[:, 0:640].rearrange("p (t d) -> p t d", d=128)), waits=[tpt])
                            tp_free = c2
                            last = [c2]
                        pending = post
                if pending is not None:
                    pending()
                fin = fl(last, set_free)
                if debug:
                    da = cx.dma("sp", lambda e: e.dma_start(out=dbg["kT"], in_=kT[:].rearrange("p k t -> p (k t)")),
                                dC, waits=fin)
                    db = cx.dma("sp", lambda e: e.dma_start(out=dbg["v"], in_=vtok[:].rearrange("p k t -> p (k t)")),
                                dD, waits=fin)
                    cx.wait_only("sp", [da, db])
                ring.prefetch(wv_in, 0)
                ring.prefetch(wv_in, V_END)
                ring.prefetch(wv_in, 128)
                cx.run()
            if nphase <= 1:
                return nc

            with ExitStack() as ps:
                psb = lambda name, shape, dt: ps.enter_context(nc.sbuf_tensor(name, shape, dt))
                qT = [psb("qT%d" % i, [128, TOK], BF16) for i in range(2)]
                q32 = psb("q32", [128, TOK], F32)
                qsw = psb("qsw", [32, TOK], F32)
                sg = [psb("sg%d" % i, [128, TOK], F32) for i in range(2)]
                pT = [psb("pT%d" % i, [128, 384], BF16) for i in range(3)]
                sgt = psb("sgt", [128, TOK], F32)
                sgt_free = None
                rden = psb("rden", [128, 512], F32)
                agf = psb("agf", [128, 512], F32)
                agout = [psb("agout%d" % i, [128, TOK], BF16) for i in range(2)]
                cx.begin()
                t_es = cx.op("act", lambda e: e.activation(esink[:], sinkb[:], AF.Exp))
                A = (0, 1)
                B = (2, 3)
                A_free = None
                B_free = None
                q_ready = [None, None]
                q_free = [None, None]
                sg_ready = [None, None]
                sg_free = [None, None]
                rot_free = None
                s_free = [None, None]
                pT_free = [None, None, None]
                acc_free = None
                ag_free = [None, None]
                cnt = {"s": 0, "p": 0}
                spill_tix = []

                rden_free = None
                agf_free = None

                def attention(h):
                    nonlocal acc_free, rden_free, agf_free
                    g = h // 4
                    hb = h % 2
                    last_s = None
                    for hf in range(2):
                        tiles = list(range(4 * hf, 4 * hf + 6))
                        info = {}

                        def emit_S(t):
                            nonlocal last_s
                            n_lo = max(t - 2, 4 * hf)
                            n_hi = min(t, 4 * hf + 3)
                            N = (n_hi - n_lo + 1) * 128
                            si = cnt["s"] % 2
                            cnt["s"] += 1
                            pi = cnt["p"] % 3
                            cnt["p"] += 1
                            sb_ = banks[4 + si]
                            mlist = []
                            for n in range(n_lo, n_hi + 1):
                                mi = None
                                if n == t:
                                    mi = 0 if t == 0 else 1
                                elif n == t - 2:
                                    mi = 3 if t == 9 else 2
                                if mi is not None and bis != 23:
                                    mlist.append(((n - n_lo) * 128, mi))
                            ts = cx.op("pe", lambda e, sb_=sb_, t=t, n_lo=n_lo, N=N, nm=len(mlist): e.matmul(
                                sb_[:, 0:N], kT[:, g, t * 128:(t + 1) * 128], qT[hb][:, n_lo * 128:n_lo * 128 + N],
                                start=True, stop=True), waits=fl(q_ready[hb], s_free[si]), signal=(len(mlist) == 0))
                            for k_, (off, mi) in enumerate(mlist):
                                lastm = (k_ == len(mlist) - 1)
                                ts = cx.op("pe", lambda e, sb_=sb_, off=off, mi=mi, lastm=lastm: e.matmul(
                                    sb_[:, off:off + 128], identb[:], masks[:, mi * 128:(mi + 1) * 128],
                                    start=False, stop=True, skip_group_check=True), signal=lastm)
                            last_s = ts
                            te = cx.op("act", lambda e, sb_=sb_, pi=pi, N=N: e.activation(
                                pT[pi][:, 0:N], sb_[:, 0:N], AF.Exp, scale=SCALE), waits=fl(ts, pT_free[pi], t_es))
                            s_free[si] = te
                            tm = te
                            info[t] = (pi, n_lo, N, te, tm)

                        def emit_PV(t):
                            nonlocal acc_free
                            pi, n_lo, N, te, tm = info[t]
                            c0 = (n_lo - 4 * hf) * 128
                            first = (t == tiles[0])
                            lastt = (t == tiles[-1])
                            cx.op("pe", lambda e, t=t, pi=pi, c0=c0, N=N, first=first, lastt=lastt: e.matmul(
                                banks[6][:, c0:c0 + N], vtok[:, t, g * 128:(g + 1) * 128], pT[pi][:, 0:N],
                                start=first, stop=True, skip_group_check=True),
                                waits=fl(te, tm, acc_free if first else None), signal=False)
                            td = cx.op("pe", lambda e, pi=pi, c0=c0, N=N, first=first, lastt=lastt: e.matmul(
                                banks[7][:, c0:c0 + N], onesb[:], pT[pi][:, 0:N],
                                start=first, stop=True, skip_group_check=True), signal=True)
                            pT_free[pi] = td
                            return td

                        emit_S(tiles[0])
                        yield
                        td = None
                        for i, t in enumerate(tiles):
                            if i + 1 < len(tiles):
                                emit_S(tiles[i + 1])
                            td = emit_PV(t)
                            yield
                        e1 = cx.op("act", lambda e, h=h: e.activation(rden[:], banks[7][:], AF.Identity if bis == 22 else AF.Ln,
                                                                       bias=esink[:, h:h + 1]),
                                   waits=fl(td, t_es, rden_free))
                        e2 = cx.op("act", lambda e: e.activation(rden[:], rden[:], AF.Exp, scale=-1.0), waits=[e1])
                        e3 = cx.op("dve", lambda e: e.tensor_tensor(agf[:], banks[6][:], rden[:], ALU.mult),
                                   waits=fl(e2, td, agf_free))
                        e4 = cx.op("dve", lambda e, hf=hf, hb=hb: e.tensor_tensor(
                            agout[hb][:, hf * 512:(hf + 1) * 512], agf[:], sg[hb][:, hf * 512:(hf + 1) * 512],
                            ALU.mult), waits=fl(e3, sg_ready[hb], ag_free[hb]))
                        acc_free = [e3, e1]
                        rden_free = e3
                        agf_free = e4
                    q_free[hb] = last_s
                    sg_free[hb] = e4
                    dsp = cx.dma("sp", lambda e, h=h, hb=hb: e.dma_start(out=ag_sp[h], in_=agout[hb][:]), spl[hb],
                                 waits=[e4])
                    ag_free[hb] = dsp
                    spill_tix.append(dsp)

                for h in range(NQH + 1):
                    gen = attention(h - 1) if h >= 1 else None
                    if h < NQH:
                        hb = h % 2
                        chq = [(tok0, n, A[i], 0) for i, (tok0, n) in enumerate(OWN2)]
                        tq = proj_task(cx, ring, wv_in, h * 128, rhs_h, chq, banks, fl(A_free), inter=gen)
                        ea = ed = None
                        for i in range(2):
                            ea = cx.op("act", lambda e, hb=hb, i=i: e.activation(
                                qT[hb][:, i * 512:(i + 1) * 512], banks[A[i]][:, :], AF.Copy),
                                waits=fl(tq, q_free[hb]))
                            ed = cx.op("act", lambda e, i=i: e.activation(
                                q32[:, i * 512:(i + 1) * 512], banks[A[i]][:, :], AF.Copy), waits=fl(tq, rot_free))

                        def rot_hook(hb=hb, ea=ea, ed=ed):
                            nonlocal A_free, rot_free
                            tr = None
                            for i in range(2):
                                if bis == 21:
                                    break
                                tr = cx.op("pe", lambda e, i=i: e.matmul(
                                    banks[A[i]][:, :], rmat[:], q32[:, i * 512:(i + 1) * 512], start=True, stop=True),
                                    waits=fl(ea, ed, ct), signal=(i == 1))
                            ev = None
                            for i in range(2):
                                ev = cx.op("act", lambda e, i=i: e.activation(
                                    qsw[:, i * 512:(i + 1) * 512], banks[A[i]][0:32, :], AF.Copy), waits=fl(tr, rot_free))
                            o1 = cx.op("dve", lambda e: e.tensor_tensor(q32[0:32, :], q32[0:32, :], cosT[:, 128:128 + TOK],
                                                                        ALU.mult), waits=fl(tr, ed, rot_free))
                            o2 = cx.op("dve", lambda e: e.tensor_tensor(qsw[:], qsw[:], sinS[:, 128:128 + TOK], ALU.mult),
                                       waits=fl(ev, rot_free))
                            o3 = cx.op("dve", lambda e, hb=hb: e.tensor_tensor(qT[hb][0:32, :], q32[0:32, :], qsw[:], ALU.add),
                                       waits=fl(o1, o2, ea, q_free[hb]))
                            rot_free = o3
                            A_free = [ea, ed, ev]
                            q_ready[hb] = [o3, ea]
                        A_free = [ea, ed]
                        q_ready[hb] = None

                        def g_inter(gen=gen, hook=rot_hook):
                            step = 0
                            done = False
                            while True:
                                if step == 1:
                                    hook()
                                if gen is not None and not done:
                                    try:
                                        next(gen)
                                    except StopIteration:
                                        done = True
                                step += 1
                                yield
                        ginter = g_inter()
                    if h < NQH:
                        chg = [(tok0, n, B[i], 0) for i, (tok0, n) in enumerate(OWN2)]
                        tg = proj_task(cx, ring, wv_in, V_END + h * 128, rhs_h, chg, banks, fl(B_free), inter=ginter)
                        ea = None
                        eb = None
                        for i in range(2):
                            a1 = cx.op("act", lambda e, i=i: e.activation(
                                sgt[:, i * 512:(i + 1) * 512], banks[B[i]][:, :], AF.Exp, scale=-1.0),
                                waits=fl(tg, sgt_free))
                            a2 = cx.op("act", lambda e, i=i: e.activation(
                                sgt[:, i * 512:(i + 1) * 512], sgt[:, i * 512:(i + 1) * 512], AF.Ln, bias=1.0),
                                waits=[a1])
                            ea = cx.op("act", lambda e, i=i: e.activation(
                                sgt[:, i * 512:(i + 1) * 512], sgt[:, i * 512:(i + 1) * 512], AF.Exp, scale=-1.0),
                                waits=[a2])
                            eb = cx.op("dve", lambda e, hb=hb, i=i: e.tensor_tensor(
                                sg[hb][:, i * 512:(i + 1) * 512], banks[B[i]][:, :], sgt[:, i * 512:(i + 1) * 512],
                                ALU.mult), waits=fl(ea, sg_free[hb]))
                        sgt_free = eb
                        B_free = [ea, eb]
                        sg_ready[hb] = eb
                    if gen is not None:
                        for _ in gen:
                            pass
                cx.wait_only("sp", spill_tix[-2:])
                ring.prefetch(wv_in, CB_END)
                ring.prefetch(wv_in, CC_END)
                ring.prefetch(wv_in, AG_END)
                cx.run()
        if nphase <= 2:
            return nc

        with ExitStack() as ps:
            psb = lambda name, shape, dt: ps.enter_context(nc.sbuf_tensor(name, shape, dt))
            Csb = psb("Csb", [128, 1026], F32)
            u = psb("u", [128, 1026], F32)
            c0t = psb("c0t", [128, TOK], F32)
            c1t = psb("c1t", [128, TOK], F32)
            c2t = psb("c2t", [128, TOK], F32)
            tB = psb("tB", [128, TOK], F32)
            sgc = psb("sgc", [128, TOK], F32)
            cbout = [psb("cbout%d" % i, [128, TOK], BF16) for i in range(2)]
            cx.begin()
            PA = (0, 1, 2)
            PB = (3, 4, 5)
            PA_free = None
            PB_free = None
            CH342 = [(127, 342), (469, 342), (811, 342)]
            u_free = None
            c2_free = None
            tB_free = None
            sgc_free = None
            cb_free = [None, None]
            spill_tix = []
            for c in range(KC):
                ch = [(tok0, n, PA[i], 0) for i, (tok0, n) in enumerate(CH342)]
                tC = proj_task(cx, ring, wv_in, CB_END + c * 128, rhs_h, ch, banks, fl(PA_free))
                ea = None
                for i in range(3):
                    ea = cx.op("act", lambda e, i=i: e.activation(Csb[:, i * 342:(i + 1) * 342], banks[PA[i]][:, 0:342],
                                                                   AF.Copy), waits=fl(tC, u_free))
                PA_free = [ea]
                ch = [(tok0, n, PB[i], 0) for i, (tok0, n) in enumerate(CH342)]
                tX = proj_task(cx, ring, wv_in, CC_END + c * 128, rhs_h, ch, banks, fl(PB_free))
                ed = None
                for i in range(3):
                    ed = cx.op("dve", lambda e, i=i: e.tensor_tensor(u[:, i * 342:(i + 1) * 342],
                                                                     Csb[:, i * 342:(i + 1) * 342],
                                                                     banks[PB[i]][:, 0:342], ALU.mult),
                               waits=fl(tX, ea, c2_free))
                PB_free = [ed]
                u_free = ed
                a0 = cx.op("act", lambda e, c=c: e.activation(c0t[:], u[:, 1:1025], AF.Identity,
                                                              bias=convb[:, c:c + 1],
                                                              scale=convw[:, KC + c:KC + c + 1]), waits=fl(ed, c2_free))
                v1 = cx.op("dve", lambda e, c=c: e.scalar_tensor_tensor(c1t[:], u[:, 0:1024], convw[:, c:c + 1], c0t[:],
                                                                         ALU.mult, ALU.add), waits=fl(a0, ed, c2_free))
                v2 = cx.op("dve", lambda e, c=c: e.scalar_tensor_tensor(c2t[:], u[:, 2:1026],
                                                                         convw[:, 2 * KC + c:2 * KC + c + 1], c1t[:],
                                                                         ALU.mult, ALU.add), waits=fl(v1, tB_free))
                u_free = v2
                ch = [(tok0, n, PA[i], 0) for i, (tok0, n) in enumerate(OWN2)]
                tBk = proj_task(cx, ring, wv_in, AG_END + c * 128, rhs_h, ch, banks, fl(PA_free))
                vb = None
                for i in range(2):
                    vb = cx.op("dve", lambda e, i=i: e.tensor_tensor(tB[:, i * 512:(i + 1) * 512], banks[PA[i]][:, :],
                                                                     c2t[:, i * 512:(i + 1) * 512], ALU.mult),
                               waits=fl(tBk, v2, tB_free))
                PA_free = [vb]
                c2_free = vb
                ch = [(tok0, n, PB[i], 0) for i, (tok0, n) in enumerate(OWN2)]
                tG = proj_task(cx, ring, wv_in, CX_END + c * 128, rhs_h, ch, banks, fl(PB_free))
                eg = None
                for i in range(2):
                    eg = cx.op("act", lambda e, i=i: e.activation(sgc[:, i * 512:(i + 1) * 512], banks[PB[i]][:, :],
                                                                   AF.Silu), waits=fl(tG, sgc_free))
                PB_free = [eg]
                vr = cx.op("dve", lambda e, c=c: e.tensor_tensor(cbout[c % 2][:], tB[:], sgc[:], ALU.mult),
                           waits=fl(eg, vb, cb_free[c % 2]))
                tB_free = vr
                sgc_free = vr
                dsp = cx.dma("sp", lambda e, c=c: e.dma_start(out=cb_sp[c], in_=cbout[c % 2][:]), spl[c % 2], waits=[vr])
                cb_free[c % 2] = dsp
                spill_tix.append(dsp)
            cx.wait_only("sp", spill_tix[-2:])
            ring.prefetch(wv_in, CG_END)
            ring.prefetch(wv_in, CG_END + 128)
            ring.prefetch(wv_a, 0)
            cx.run()
        if nphase <= 3:
            return nc

        PAIRS = [(0, 1), (2, 3), (4, 5), (6, 7)]
        OWNL = [(0, 512), (512, 512)]
        with ExitStack() as ps:
            psb = lambda name, shape, dt: ps.enter_context(nc.sbuf_tensor(name, shape, dt))
            actT = psb("agT", [128, KC, TOK], BF16)
            gA = [psb("gA%d" % i, [128, TOK], F32) for i in range(2)]
            mst = [psb("mst%d" % i, [128, TOK], F32) for i in range(2)]
            cx.begin()
            lds = []
            for q in range(8):
                lds.append(cx.dma("sp", lambda e, q=q: e.dma_start(
                    out=actT[:, q * 4:(q + 1) * 4, :], in_=ag_sp[q * 4:(q + 1) * 4].rearrange("j p t -> p j t")), ldq[q]))
            rhs_a = lambda kc, tok0, n: actT[:, kc, tok0:tok0 + n]
            pair_free = [None] * 4
            gA_free = [None, None]
            mst_free = [None, None]
            spill_tix = []
            np_ = [0]
            gA_ready = [None, None]

            def emit_L(j):
                ib = np_[0] % 4
                pb = PAIRS[ib]
                np_[0] += 1
                ch = [(tok0, n, pb[i], 0) for i, (tok0, n) in enumerate(OWN2)]
                tL = proj_task(cx, ring, wv_in, CG_END + j * 128, rhs_h, ch, banks, fl(pair_free[ib]))
                ea = None
                for i in range(2):
                    ea = cx.op("act", lambda e, i=i, j=j, pb=pb: e.activation(
                        gA[j % 2][:, i * 512:(i + 1) * 512], banks[pb[i]][:, :], AF.Sigmoid, bias=bm[:, j:j + 1]),
                        waits=fl(tL, gA_free[j % 2]))
                pair_free[ib] = [ea]
                gA_ready[j % 2] = ea

            def emit_Y(j):
                ia = np_[0] % 4
                pa = PAIRS[ia]
                np_[0] += 1
                ch = [(tok0, n, pa[i], 0) for i, (tok0, n) in enumerate(OWNL)]
                tY = proj_task(cx, ring, wv_a, j * 128, rhs_a, ch, banks, fl(pair_free[ia]),
                               kc_waits=({q * 4: [lds[q]] for q in range(8)} if j == 0 else None))
                vm = None
                for i in range(2):
                    vm = cx.op("dve", lambda e, i=i, j=j, pa=pa: e.tensor_tensor(
                        mst[j % 2][:, i * 512:(i + 1) * 512], banks[pa[i]][:, :], gA[j % 2][:, i * 512:(i + 1) * 512],
                        ALU.mult), waits=fl(tY, gA_ready[j % 2], mst_free[j % 2]))
                pair_free[ia] = [vm]
                gA_free[j % 2] = vm
                dsp = cx.dma("sp", lambda e, j=j: e.dma_start(out=ma_sp[j], in_=mst[j % 2][:]), spl[j % 2], waits=[vm])
                mst_free[j % 2] = dsp
                spill_tix.append(dsp)

            emit_L(0)
            emit_L(1)
            for j in range(KC):
                emit_Y(j)
                if j + 2 < KC:
                    emit_L(j + 2)
            cx.wait_only("sp", spill_tix[-2:])
            ring.prefetch(wv_in, MA_END)
            ring.prefetch(wv_in, MA_END + 128)
            ring.prefetch(wv_b, 0)
            cx.run()
        if nphase <= 4:
            return nc

        with ExitStack() as ps:
            psb = lambda name, shape, dt: ps.enter_context(nc.sbuf_tensor(name, shape, dt))
            actT = psb("cbT", [128, KC, TOK], BF16)
            gB = [psb("gB%d" % i, [128, TOK], F32) for i in range(2)]
            mAin = [psb("mAin%d" % i, [128, TOK], F32) for i in range(2)]
            tYb = psb("tYb", [128, TOK], F32)
            mout = [psb("mout%d" % i, [128, TOK], BF16) for i in range(2)]
            cx.begin()
            lds = []
            for q in range(8):
                lds.append(cx.dma("sp", lambda e, q=q: e.dma_start(
                    out=actT[:, q * 4:(q + 1) * 4, :], in_=cb_sp[q * 4:(q + 1) * 4].rearrange("j p t -> p j t")), ldq[q]))
            rhs_a = lambda kc, tok0, n: actT[:, kc, tok0:tok0 + n]
            pair_free = [None] * 4
            gB_free = [None, None]
            mAin_free = [None, None]
            mout_free = [None, None]
            tYb_free = None
            msem = [dB, dC]
            spill_tix = []
            np_ = [0]
            gB_ready = [None, None]

            def emit_L(j):
                ib = np_[0] % 4
                pb = PAIRS[ib]
                np_[0] += 1
                ch = [(tok0, n, pb[i], 0) for i, (tok0, n) in enumerate(OWN2)]
                tL = proj_task(cx, ring, wv_in, MA_END + j * 128, rhs_h, ch, banks, fl(pair_free[ib]))
                ea = None
                for i in range(2):
                    ea = cx.op("act", lambda e, i=i, j=j, pb=pb: e.activation(
                        gB[j % 2][:, i * 512:(i + 1) * 512], banks[pb[i]][:, :], AF.Sigmoid,
                        bias=bm[:, 32 + j:33 + j]), waits=fl(tL, gB_free[j % 2]))
                pair_free[ib] = [ea]
                gB_ready[j % 2] = ea

            def emit_Y(j):
                nonlocal tYb_free
                ia = np_[0] % 4
                pa = PAIRS[ia]
                np_[0] += 1
                lm = cx.dma("sp", lambda e, j=j: e.dma_start(out=mAin[j % 2][:], in_=ma_sp[j]), msem[j % 2],
                            waits=fl(mAin_free[j % 2]))
                ch = [(tok0, n, pa[i], 0) for i, (tok0, n) in enumerate(OWNL)]
                tY = proj_task(cx, ring, wv_b, j * 128, rhs_a, ch, banks, fl(pair_free[ia]),
                               kc_waits=({q * 4: [lds[q]] for q in range(8)} if j == 0 else None))
                vm = None
                for i in range(2):
                    vm = cx.op("dve", lambda e, i=i, j=j, pa=pa: e.tensor_tensor(
                        tYb[:, i * 512:(i + 1) * 512], banks[pa[i]][:, :], gB[j % 2][:, i * 512:(i + 1) * 512],
                        ALU.mult), waits=fl(tY, gB_ready[j % 2], tYb_free))
                pair_free[ia] = [vm]
                gB_free[j % 2] = vm
                va = cx.op("dve", lambda e, j=j: e.tensor_tensor(mout[j % 2][:], tYb[:], mAin[j % 2][:], ALU.add),
                           waits=fl(vm, lm, mout_free[j % 2]))
                tYb_free = va
                mAin_free[j % 2] = va
                dsp = cx.dma("sp", lambda e, j=j: e.dma_start(out=m_sp[j], in_=mout[j % 2][:]), spl[j % 2], waits=[va])
                mout_free[j % 2] = dsp
                spill_tix.append(dsp)

            emit_L(0)
            emit_L(1)
            for j in range(KC):
                emit_Y(j)
                if j + 2 < KC:
                    emit_L(j + 2)
            cx.wait_only("sp", spill_tix[-2:])
            for j_ in range(3):
                ring.prefetch(wv_o, j_ * 128)
            cx.run()
        if nphase <= 5:
            return nc
        hs.close()
        xp = [es.enter_context(nc.sbuf_tensor("xp%d" % i, [128, D], F32)) for i in range(4)]
        xps = [cx.new_sem("xps%d" % i) for i in range(4)]
        xpt = [None] * 4

        with ExitStack() as ps:
            psb = lambda name, shape, dt: ps.enter_context(nc.sbuf_tensor(name, shape, dt))
            actT = psb("mT", [128, KC, TOK], BF16)
            oT_sb = [psb("oTsb%d" % i, [128, TOK], F32) for i in range(2)]
            ostage = [psb("ostage%d" % i, [128, 8, 128], F32) for i in range(2)]
            junk6 = psb("junk6", [128, 128], F32)
            gpost = psb("gpost_s", [128, D], F32)
            cx.begin()
            tgp = cx.dma("sp", lambda e: e.dma_start(out=gpost[:], in_=gpost_d), csem)
            lds = []
            for q in range(8):
                lds.append(cx.dma("sp", lambda e, q=q: e.dma_start(
                    out=actT[:, q * 4:(q + 1) * 4, :], in_=m_sp[q * 4:(q + 1) * 4].rearrange("j p t -> p j t")), ldq[q]))
            rhs_a = lambda kc, tok0, n: actT[:, kc, tok0:tok0 + n]
            out_v = out_d.rearrange("(tt p) d -> p tt d", p=128)
            pair_free = [None, None]
            j6_free = [None]
            oT_free = [None, None]
            tp_free = [None, None]
            ost_free = [None, None]
            pending = None
            spill_tix = []
            for j in range(KC + 1):
                if j < KC:
                    jb = j % 2
                    pa = PAIRS[jb]
                    ch = [(tok0, n, pa[i], 0) for i, (tok0, n) in enumerate(OWNL)]
                    tO = proj_task(cx, ring, wv_o, j * 128, rhs_a, ch, banks, fl(pair_free[jb]),
                                   kc_waits=({q * 4: [lds[q]] for q in range(8)} if j == 0 else None))
                if pending is not None:
                    pending()
                    pending = None
                if j == 8:
                    for i_ in range(4):
                        xpt[i_] = cx.dma("sp", lambda e, i_=i_: e.dma_start(
                            out=xp[i_][:], in_=x_ext[(i_ + 1) * 128:(i_ + 2) * 128, :]), xps[i_])
                if j < KC:
                    ea = None
                    for i in range(2):
                        ea = cx.op("act", lambda e, i=i, jb=jb, pa=pa: e.activation(
                            oT_sb[jb][:, i * 512:(i + 1) * 512], banks[pa[i]][:, :], AF.Copy),
                            waits=fl(tO, oT_free[jb]))
                    pair_free[jb] = [ea]

                    def post(j=j, jb=jb, ea=ea):
                        tpb = (4 + 2 * jb, 5 + 2 * jb)
                        tpt = None
                        for tt in range(8):
                            dst = banks[tpb[tt // 4]][:, (tt % 4) * 128:(tt % 4 + 1) * 128]
                            tpt = cx.op("pe", lambda e, dst=dst, tt=tt, jb=jb: e.transpose(
                                dst, oT_sb[jb][:, tt * 128:(tt + 1) * 128], identf[:]),
                                waits=fl(ea, tp_free[jb]) if tt == 0 else (), signal=(tt == 7))
                        oT_free[jb] = tpt
                        sq = None
                        for tt in range(8):
                            src = banks[tpb[tt // 4]][:, (tt % 4) * 128:(tt % 4 + 1) * 128]
                            sq = cx.op("act", lambda e, src=src, tt=tt, j=j: e.activation(
                                junk6[:], src, AF.Square, accum_out=ssq[:, tt * KC + j:tt * KC + j + 1]),
                                waits=fl(tpt, j6_free[0]))
                            j6_free[0] = sq
                        cp = None
                        for tt in range(8):
                            src = banks[tpb[tt // 4]][:, (tt % 4) * 128:(tt % 4 + 1) * 128]
                            cp = cx.op("dve", lambda e, src=src, tt=tt, j=j, jb=jb: e.tensor_tensor(
                                ostage[jb][:, tt, :], src, gpost[:, j * 128:(j + 1) * 128], ALU.mult),
                                waits=fl(sq, tgp, ost_free[jb]))
                        tp_free[jb] = [cp]
                        dsp = cx.dma("sp", lambda e, j=j, jb=jb: e.dma_start(
                            out=out_v[:, :, j * 128:(j + 1) * 128], in_=ostage[jb][:]), spl[jb], waits=[cp])
                        ost_free[jb] = [dsp]
                        spill_tix.append(dsp)
                    pending = post
            cx.wait_only("sp", spill_tix[-2:])
            cx.run()
        if nphase <= 6:
            return nc

        with ExitStack() as ps:
            psb = lambda name, shape, dt: ps.enter_context(nc.sbuf_tensor(name, shape, dt))
            ot = [psb("ot%d" % i, [128, D], F32) for i in range(3)]
            xr = [psb("xr%d" % i, [128, D], F32) for i in range(2)]
            rs = psb("rs", [128, 24], F32)
            cx.begin()
            r1 = cx.op("dve", lambda e: e.reduce_sum(rs[:, 0:8], ssq[:].rearrange("p (t j) -> p t j", j=KC),
                                                     mybir.AxisListType.X))
            r2 = cx.op("dve", lambda e: e.tensor_scalar(rs[:, 8:16], rs[:, 0:8], 1.0 / D, EPS, ALU.mult, ALU.add),
                       waits=[r1])
            r3 = cx.op("act", lambda e: e.activation(rs[:, 16:24], rs[:, 8:16], AF.Sqrt), waits=[r2])
            r4 = cx.op("dve", lambda e: e.reciprocal(rstd_o[:], rs[:, 16:24]), waits=[r3])
            lsem = [dA, dB, ldq[1]]
            xsem7 = [ldq[2], ldq[3]]
            ssem = [dC, dD, ldq[4]]
            ot_free = [None, None, None]
            xr_free = [None, None]
            stores = []
            for tt in range(8):
                b = tt % 3
                l1 = cx.dma("sp", lambda e, tt=tt, b=b: e.dma_start(out=ot[b][:], in_=out_d[tt * 128:(tt + 1) * 128, :]),
                            lsem[b], waits=fl(ot_free[b]))
                if tt < 4:
                    l2 = xpt[tt]
                    xsrc = xp[tt]
                else:
                    xb = tt % 2
                    l2 = cx.dma("sp", lambda e, tt=tt, xb=xb: e.dma_start(
                        out=xr[xb][:], in_=x_ext[(tt + 1) * 128:(tt + 2) * 128, :]), xsem7[xb], waits=fl(xr_free[xb]))
                    xsrc = xr[xb]
                v1 = cx.op("dve", lambda e, tt=tt, b=b, xsrc=xsrc: e.scalar_tensor_tensor(
                    ot[b][:], ot[b][:], rstd_o[:, tt:tt + 1], xsrc[:], ALU.mult, ALU.add), waits=fl(l1, l2, r4))
                if tt >= 4:
                    xr_free[tt % 2] = v1
                s1 = cx.dma("act", lambda e, tt=tt, b=b: e.dma_start(out=out_d[tt * 128:(tt + 1) * 128, :], in_=ot[b][:]),
                            ssem[b], waits=[v1])
                ot_free[b] = s1
                stores.append(s1)
            cx.wait_only("act", stores[-3:])
            cx.wait_only("sp", stores[-3:])
            cx.run()
        return nc


def _host_prep(x, norm_pre, w_in, b_merge, attn_sink, conv_w, conv_b, w_attn_out, w_conv_out, w_out, norm_post):
    f32 = np.float32
    x2 = np.asarray(x, f32).reshape(S, D)
    w_in2 = np.ascontiguousarray(np.asarray(w_in, f32).reshape(D, IN_COLS))
    w_a2 = np.ascontiguousarray(np.asarray(w_attn_out, f32).reshape(D, D))
    w_b2 = np.ascontiguousarray(np.asarray(w_conv_out, f32).reshape(D, D))
    w_o2 = np.ascontiguousarray(np.asarray(w_out, f32).reshape(D, D))
    gpre = np.ascontiguousarray(np.asarray(norm_pre, f32).reshape(KC, 128).T)
    bm = np.ascontiguousarray(np.asarray(b_merge, f32).reshape(64, 128).T)
    sinkb = np.ascontiguousarray(np.broadcast_to(np.asarray(attn_sink, f32).reshape(1, NQH), (128, NQH)))
    cw = np.asarray(conv_w, f32).reshape(3, KC, 128)
    convw = np.ascontiguousarray(cw.transpose(2, 0, 1).reshape(128, 3 * KC))
    convb = np.ascontiguousarray(np.asarray(conv_b, f32).reshape(KC, 128).T)
    gpost = np.ascontiguousarray(np.broadcast_to(np.asarray(norm_post, f32).reshape(1, D), (128, D)))
    gprebc = np.ascontiguousarray(np.broadcast_to(np.asarray(norm_pre, f32).reshape(1, D), (128, D)))
    identf = np.eye(128, dtype=f32)
    identb = np.eye(128, dtype=f32).astype(ml_dtypes.bfloat16)
    onesb = np.ones((128, 128), dtype=f32).astype(ml_dtypes.bfloat16)
    jj = np.arange(128)[:, None]
    ii = np.arange(128)[None, :]
    NEG = np.float32(-30000.0)
    mL = np.where(jj >= ii, np.float32(0.0), NEG).astype(f32)
    mR = np.where(jj <= ii, np.float32(0.0), NEG).astype(f32)
    zero = np.full((128, 128), NEG, f32)
    rmat = np.zeros((128, 128), f32)
    for p in range(16):
        rmat[p + 16, p] = 1.0
        rmat[p, p + 16] = 1.0
    rot = 32
    inv_freq = (np.float32(500000.0) ** (-np.arange(0, rot, 2, dtype=f32) / np.float32(rot))).astype(f32)
    xpad = np.zeros((S + 256, D), f32)
    xpad[128:128 + S] = x2
    in_maps = []
    for c in range(NCORES):
        pos = (np.arange(EXT, dtype=np.int64) + c * TOK - 128)
        posf = np.clip(pos, 0, S - 1).astype(f32)
        ang = (posf[:, None] * inv_freq[None, :]).astype(f32)
        cosv = np.cos(ang).astype(f32).T
        sinv = np.sin(ang).astype(f32).T
        cosT = np.ascontiguousarray(np.concatenate([cosv, cosv], axis=0))
        sinS = np.ascontiguousarray(np.concatenate([-sinv, sinv], axis=0))
        masks = np.concatenate([zero if c == 0 else mL, mL, mR, zero if c == NCORES - 1 else mR], axis=1)
        in_maps.append({
            "x_ext": np.ascontiguousarray(xpad[c * TOK:c * TOK + EXT]),
            "w_in": w_in2, "w_a": w_a2, "w_b": w_b2, "w_o": w_o2,
            "gpre": gpre, "bm": bm, "sinkb": sinkb, "convw": convw, "convb": convb, "gpost": gpost, "gprebc": gprebc,
            "identf": identf, "identb": identb, "onesb": onesb,
            "masks": np.ascontiguousarray(masks).astype(ml_dtypes.bfloat16),
            "cosT": cosT, "sinT": sinS, "rmat": rmat,
        })
    return in_maps


def kernel(x, norm_pre, w_in, b_merge, attn_sink, conv_w, conv_b, w_attn_out, w_conv_out, w_out, norm_post):
    in_maps = _host_prep(x, norm_pre, w_in, b_merge, attn_sink, conv_w, conv_b, w_attn_out, w_conv_out, w_out,
                         norm_post)
    nc = build_nc()
    res = run_bass_kernel_spmd(nc, in_maps, core_ids=list(range(NCORES)))
    outs = [np.asarray(r["out"], dtype=np.float32) for r in res.results]
    return np.concatenate(outs, axis=0).reshape(1, S, D)
```
